# Optimizing a Trainium2 kernel written in Bass

```python
import math
import jax
import jax.numpy as jnp
from jax import lax
import numpy as np

D_MODEL = 1024
BATCH = 8
SEQ = 4096
DEPTH = 2

N_META = 16
CHUNK = 128
PAD = CHUNK - N_META
HEAD_DIM = 64
SB_HEADS = 4
SB_W = SB_HEADS * HEAD_DIM
SSD_HEADS = 8
SSD_W = SSD_HEADS * HEAD_DIM
SSD_GROUPS = 2
SSD_STATE = 128
SSD_CONV = 4
SSD_CONV_DIM = SSD_W + 2 * SSD_GROUPS * SSD_STATE
HG_HEADS = 4
HG_DK = 64
HG_DV = 64
HG_W = HG_HEADS * HG_DV
D_MIX = SB_W + SSD_W + HG_W
D_FF = 4 * D_MODEL
EPS = 1e-6
TINY = 1e-30
IN_SIZES = (SB_W, SB_W, SB_W,
            SSD_W, SSD_W, SSD_GROUPS * SSD_STATE, SSD_GROUPS * SSD_STATE, SSD_HEADS,
            HG_HEADS * HG_DK, HG_HEADS * HG_DK, HG_W, HG_W)
D_IN = 3 * SB_W + 2 * SSD_W + 2 * SSD_GROUPS * SSD_STATE + SSD_HEADS + 2 * HG_HEADS * HG_DK + 2 * HG_W

kernel_name = 'hymba_sb_ssd_hgrn2_block'


def rmsnorm(x, w):
    xf = x.astype(jnp.float32)
    y = xf * lax.rsqrt(jnp.mean(xf * xf, axis=-1, keepdims=True) + EPS)
    return (y * w.astype(jnp.float32)).astype(x.dtype)


def causal_depthwise_conv(u, w, b):
    y = lax.conv_general_dilated(
        u, w[:, None, :].astype(u.dtype), window_strides=(1,), padding=[(w.shape[0] - 1, 0)],
        dimension_numbers=('NWC', 'WIO', 'NWC'), feature_group_count=u.shape[-1])
    return y + b.astype(u.dtype)


def to_chunks(t):
    b, l = t.shape[:2]
    return jnp.moveaxis(t.reshape((b, l // CHUNK, CHUNK) + t.shape[2:]), 1, 0)


def from_chunks(t):
    n, b = t.shape[:2]
    return jnp.moveaxis(t, 0, 1).reshape((b, n * CHUNK) + t.shape[3:])


def masked_decay(seg, mask):
    return jnp.where(mask, jnp.exp(jnp.where(mask, seg, 0.0)), 0.0)


def stick_breaking_attention(q, k, v, valid):
    L, dh = q.shape[2], q.shape[3]
    scale = dh ** -0.5
    pos = jnp.arange(L)
    outs = []
    for blk in range(L // CHUNK):
        start, end = blk * CHUNK, (blk + 1) * CHUNK
        z = jnp.einsum('bhqd,bhkd->bhqk', q[:, :, start:end], k[:, :, :end]).astype(jnp.float32) * scale
        mask = (pos[None, :end] < pos[start:end, None]) & valid[None, :end]
        log_keep = jnp.where(mask, jax.nn.log_sigmoid(-z), 0.0)
        csum = jnp.cumsum(log_keep, axis=-1)
        log_w = jax.nn.log_sigmoid(z) + (csum[..., -1:] - csum)
        w = jnp.where(mask, jnp.exp(jnp.where(mask, log_w, 0.0)), 0.0)
        outs.append(jnp.einsum('bhqk,bhkd->bhqd', w.astype(v.dtype), v[:, :, :end]))
    return jnp.concatenate(outs, axis=2)


def stick_breaking_group(q, k, v, q_norm, k_norm, out_norm, valid):
    bsz, L, _ = q.shape
    shp = (bsz, L, SB_HEADS, HEAD_DIM)
    qh = jnp.transpose(rmsnorm(q.reshape(shp), q_norm), (0, 2, 1, 3))
    kh = jnp.transpose(rmsnorm(k.reshape(shp), k_norm), (0, 2, 1, 3))
    vh = jnp.transpose(v.reshape(shp), (0, 2, 1, 3))
    o = jnp.transpose(stick_breaking_attention(qh, kh, vh, valid), (0, 2, 1, 3))
    return rmsnorm(o, out_norm).reshape(bsz, L, SB_W)


def ssd_chunked(xdt, a, bm, cm):
    bsz = xdt.shape[0]
    rep = SSD_HEADS // SSD_GROUPS
    causal = jnp.tril(jnp.ones((CHUNK, CHUNK), dtype=bool))

    def step(state, inp):
        a_c, x_c, b_c, c_c = inp
        acum = jnp.cumsum(a_c, axis=1)
        seg = acum[:, :, None, :] - acum[:, None, :, :]
        decay = masked_decay(seg, causal[None, :, :, None])
        cb = jnp.repeat(jnp.einsum('btgn,bsgn->btsg', c_c, b_c), rep, axis=-1)
        y_diag = jnp.einsum('btsh,bshp->bthp', cb * decay, x_c)
        c_h = jnp.repeat(c_c, rep, axis=2)
        b_h = jnp.repeat(b_c, rep, axis=2)
        y_off = jnp.einsum('bthn,bhpn->bthp', c_h, state) * jnp.exp(acum)[..., None]
        w_end = jnp.exp(acum[:, -1:, :] - acum)
        state = state * jnp.exp(acum[:, -1, :])[:, :, None, None] + jnp.einsum('bshn,bsh,bshp->bhpn', b_h, w_end, x_c)
        return state, y_diag + y_off

    state0 = jnp.zeros((bsz, SSD_HEADS, HEAD_DIM, SSD_STATE), jnp.float32)
    xs = tuple(to_chunks(t.astype(jnp.float32)) for t in (a, xdt, bm, cm))
    _, y = lax.scan(step, state0, xs)
    return from_chunks(y)


def ssd_group(z, xs, bm, cm, dt_raw, conv_w, conv_b, dt_bias, a_log, d_skip, norm_w, vmask):
    bsz, L, _ = xs.shape
    f32 = jnp.float32
    xbc = jax.nn.silu(causal_depthwise_conv(jnp.concatenate([xs, bm, cm], axis=-1) * vmask, conv_w, conv_b))
    xs, bm, cm = jnp.split(xbc, [SSD_W, SSD_W + SSD_GROUPS * SSD_STATE], axis=-1)
    dt = jax.nn.softplus(dt_raw.astype(f32) + dt_bias.astype(f32)) * vmask.astype(f32)
    a_neg = -jnp.exp(a_log.astype(f32))
    xh = xs.reshape(bsz, L, SSD_HEADS, HEAD_DIM).astype(f32)
    y = ssd_chunked(xh * dt[..., None], dt * a_neg,
                    bm.reshape(bsz, L, SSD_GROUPS, SSD_STATE), cm.reshape(bsz, L, SSD_GROUPS, SSD_STATE))
    y = y + xh * d_skip.astype(f32)[:, None]
    y = y.reshape(bsz, L, SSD_W) * jax.nn.silu(z.astype(f32))
    y = rmsnorm(y.reshape(bsz, L, SSD_GROUPS, SSD_W // SSD_GROUPS), norm_w)
    return y.reshape(bsz, L, SSD_W).astype(z.dtype)


def hgrn2_chunked(q, k, v, log_f):
    bsz = q.shape[0]
    causal = jnp.tril(jnp.ones((CHUNK, CHUNK), dtype=bool))

    def step(S, inp):
        q_c, k_c, v_c, g_c = inp
        gcum = jnp.cumsum(g_c, axis=1)
        seg = gcum[:, :, None] - gcum[:, None, :]
        decay = masked_decay(seg, causal[None, :, :, None, None])
        scores = jnp.einsum('bthk,btshk,bshk->btsh', q_c, decay, k_c)
        o_intra = jnp.einsum('btsh,bshv->bthv', scores, v_c)
        o_inter = jnp.einsum('bthk,bhkv->bthv', q_c * jnp.exp(gcum), S)
        k_end = k_c * jnp.exp(gcum[:, -1:] - gcum)
        S = S * jnp.exp(gcum[:, -1])[..., None] + jnp.einsum('bshk,bshv->bhkv', k_end, v_c)
        return S, o_intra + o_inter

    S0 = jnp.zeros((bsz, HG_HEADS, HG_DK, HG_DV), jnp.float32)
    _, o = lax.scan(step, S0, tuple(to_chunks(t) for t in (q, k, v, log_f)))
    return from_chunks(o)


def hgrn2_group(q, f_logit, i_in, g, lb, out_norm, valid):
    bsz, L, _ = q.shape
    f32 = jnp.float32
    fl = f_logit.astype(f32)
    keep = valid[None, :, None]
    f = lb + (1.0 - lb) * jax.nn.sigmoid(fl)
    log_f = jnp.where(keep, jnp.log(jnp.maximum(f, TINY)), 0.0)
    k = jnp.where(keep, (1.0 - lb) * jax.nn.sigmoid(-fl), 0.0)
    v = jnp.where(keep, i_in.astype(f32), 0.0)
    qf = jax.nn.silu(q.astype(f32))
    shp = (bsz, L, HG_HEADS, HG_DK)
    o = hgrn2_chunked(qf.reshape(shp), k.reshape(shp), v.reshape(bsz, L, HG_HEADS, HG_DV), log_f.reshape(shp))
    o = rmsnorm(o, out_norm) * jax.nn.silu(g.astype(f32)).reshape(bsz, L, HG_HEADS, HG_DV)
    return o.reshape(bsz, L, HG_W).astype(q.dtype)


def setup_inputs(seed: int = 0) -> dict:
    key = jax.random.key(seed)
    ks = jax.random.split(key, 19)
    f32 = jnp.float32

    def normal(k, shape, scale):
        return scale * jax.random.normal(k, shape, f32)

    def gain(k, shape):
        return 1.0 + 0.01 * jax.random.normal(k, shape, f32)

    dt_init = jnp.exp(jax.random.uniform(ks[10], (DEPTH, SSD_HEADS), f32,
                                         minval=math.log(1e-3), maxval=math.log(1e-1)))
    return {
        'x': normal(ks[0], (BATCH, SEQ, D_MODEL), 1.0),
        'meta_tokens': normal(ks[1], (N_META, D_MODEL), 1.0),
        'hg_lb_logits': normal(ks[2], (DEPTH, HG_HEADS * HG_DK), 0.5),
        'norm_mix_w': gain(ks[3], (DEPTH, D_MODEL)),
        'w_in': normal(ks[4], (DEPTH, D_MODEL, D_IN), D_MODEL ** -0.5),
        'sb_q_norm': gain(ks[5], (DEPTH, HEAD_DIM)),
        'sb_k_norm': gain(ks[6], (DEPTH, HEAD_DIM)),
        'sb_out_norm': gain(ks[7], (DEPTH, SB_HEADS, HEAD_DIM)),
        'ssd_conv_w': normal(ks[8], (DEPTH, SSD_CONV, SSD_CONV_DIM), SSD_CONV ** -0.5),
        'ssd_conv_b': normal(ks[9], (DEPTH, SSD_CONV_DIM), 0.01),
        'ssd_dt_bias': dt_init + jnp.log(-jnp.expm1(-dt_init)),
        'ssd_A_log': jnp.log(jax.random.uniform(ks[11], (DEPTH, SSD_HEADS), f32, minval=1.0, maxval=16.0)),
        'ssd_D': gain(ks[12], (DEPTH, SSD_HEADS)),
        'ssd_norm_w': gain(ks[13], (DEPTH, SSD_GROUPS, SSD_W // SSD_GROUPS)),
        'hg_out_norm': gain(ks[14], (DEPTH, HG_HEADS, HG_DV)),
        'w_out': normal(ks[15], (DEPTH, D_MIX, D_MODEL), D_MIX ** -0.5),
        'norm_mlp_w': gain(ks[16], (DEPTH, D_MODEL)),
        'w_up': normal(ks[17], (DEPTH, D_MODEL, D_FF), D_MODEL ** -0.5),
        'w_down': normal(ks[18], (DEPTH, D_FF, D_MODEL), D_FF ** -0.5),
    }


def reference(x, meta_tokens, hg_lb_logits, norm_mix_w, w_in, sb_q_norm, sb_k_norm, sb_out_norm,
              ssd_conv_w, ssd_conv_b, ssd_dt_bias, ssd_A_log, ssd_D, ssd_norm_w, hg_out_norm,
              w_out, norm_mlp_w, w_up, w_down):
    bsz = x.shape[0]
    dtype = x.dtype
    lead = jnp.concatenate([jnp.zeros((PAD, D_MODEL), dtype), meta_tokens.astype(dtype)], axis=0)
    h = jnp.concatenate([jnp.broadcast_to(lead[None], (bsz, CHUNK, D_MODEL)), x], axis=1)
    L = h.shape[1]
    valid = jnp.arange(L) >= PAD
    vmask = valid[None, :, None].astype(dtype)
    probs = jax.nn.softmax(hg_lb_logits.astype(jnp.float32), axis=0)
    lbs = jnp.concatenate([jnp.zeros_like(probs[0:1]), jnp.cumsum(probs, axis=0)[:-1]], axis=0)
    split_at = np.cumsum(IN_SIZES)[:-1].tolist()
    for l in range(DEPTH):
        hn = rmsnorm(h, norm_mix_w[l])
        proj = hn @ w_in[l]
        (q_sb, k_sb, v_sb, z_ssd, x_ssd, b_ssd, c_ssd, dt_ssd,
         q_hg, f_hg, i_hg, g_hg) = jnp.split(proj, split_at, axis=-1)
        o_sb = stick_breaking_group(q_sb, k_sb, v_sb, sb_q_norm[l], sb_k_norm[l], sb_out_norm[l], valid)
        o_ssd = ssd_group(z_ssd, x_ssd, b_ssd, c_ssd, dt_ssd, ssd_conv_w[l], ssd_conv_b[l], ssd_dt_bias[l],
                          ssd_A_log[l], ssd_D[l], ssd_norm_w[l], vmask)
        o_hg = hgrn2_group(q_hg, f_hg, i_hg, g_hg, lbs[l], hg_out_norm[l], valid)
        h = h + jnp.concatenate([o_sb, o_ssd, o_hg], axis=-1) @ w_out[l]
        hn = rmsnorm(h, norm_mlp_w[l])
        h = h + jnp.square(jax.nn.relu(hn @ w_up[l])) @ w_down[l]
    return h[:, CHUNK:]
```

```python
import numpy as np
from contextlib import ExitStack
import concourse.bass as bass
import concourse.mybir as mybir
from concourse.bass_utils import run_bass_kernel_spmd

F32 = mybir.dt.float32
BF16 = mybir.dt.bfloat16
I32 = mybir.dt.int32
AF = mybir.ActivationFunctionType
ALU = mybir.AluOpType
AX = mybir.AxisListType

D = 1024
DIN = 3336
DFF = 4096
EPS = 1e-6
TINY = 1e-30
PADN = 112
ENGS = ("pe", "act", "dve", "pool", "sp")


class Op:
    __slots__ = ("eng", "fn", "deps", "signal", "sigval", "dma_key", "dma_val", "waits")

    def __init__(self, eng, fn, dma_key=None):
        self.eng = eng
        self.fn = fn
        self.deps = []
        self.signal = False
        self.sigval = None
        self.dma_key = dma_key
        self.dma_val = None
        self.waits = None


class Sched:
    def __init__(self, nc):
        self.nc = nc
        self.ops = {e: [] for e in ENGS}
        self.last_w = {}
        self.readers = {}
        self.dma_keys = {}
        self.last_eng = {}
        self.last_dma = {}
        self.pending_barrier = {}

    def _same_ok(self, e):
        return e == "pe"

    def op(self, eng, fn, reads=(), writes=(), dma_key=None):
        o = Op(eng, fn, dma_key)
        deps = []
        for r in reads:
            w = self.last_w.get(r)
            if w is not None:
                deps.append(w)
            if r.startswith("bank"):
                deps.extend(x for x in self.readers.get(r, ()) if x.eng != eng)
        for r in writes:
            w = self.last_w.get(r)
            if w is not None:
                deps.append(w)
            deps.extend(self.readers.get(r, ()))
        if eng in self.pending_barrier:
            deps.extend(self.pending_barrier.pop(eng))
        o.deps = deps
        for r in writes:
            self.last_w[r] = o
            self.readers[r] = []
        for r in reads:
            self.readers.setdefault(r, []).append(o)
        if dma_key is not None:
            v = self.dma_keys.get(dma_key, 0) + 16
            self.dma_keys[dma_key] = v
            o.dma_val = v
            self.last_dma[dma_key] = o
        else:
            self.last_eng[eng] = o
        self.ops[eng].append(o)
        return o

    def barrier(self):
        deps = list(self.last_eng.values()) + list(self.last_dma.values())
        for e in ENGS:
            self.pending_barrier[e] = list(deps)

    def finalize(self):
        for e in ENGS:
            for o in self.ops[e]:
                for d in o.deps:
                    if d is o or d.dma_key is not None:
                        continue
                    if d.eng != e or not self._same_ok(e):
                        d.signal = True
        self.nsig = {}
        for e in ENGS:
            c = 0
            for o in self.ops[e]:
                if o.signal and o.dma_key is None:
                    c += 1
                    o.sigval = c
            self.nsig[e] = c
        for e in ENGS:
            known = {}
            for o in self.ops[e]:
                w = {}
                for d in o.deps:
                    if d is o:
                        continue
                    if d.dma_key is not None:
                        k = ("dma", d.dma_key)
                        v = d.dma_val
                    else:
                        if d.eng == e and self._same_ok(e):
                            continue
                        k = ("eng", d.eng)
                        v = d.sigval
                    if known.get(k, 0) >= v:
                        continue
                    if w.get(k, 0) < v:
                        w[k] = v
                for k, v in w.items():
                    known[k] = v
                o.waits = w

    def emit(self, stack):
        nc = self.nc
        self.finalize()
        EPOCH = 30000
        sems = {}
        for e in ENGS:
            n = max((self.nsig[e] + EPOCH - 1) // EPOCH, 1)
            for i in range(n):
                sems[("eng", e, i)] = stack.enter_context(nc.semaphore(f"s_{e}_{i}"))
        for k in self.dma_keys:
            sems[("dma", k)] = stack.enter_context(nc.semaphore(f"d_{k}"))
        block = stack.enter_context(nc.Block())

        def run(e, eng):
            for o in self.ops[e]:
                for k, v in o.waits.items():
                    if k[0] == "dma":
                        eng.wait_ge(sems[k], v)
                    else:
                        ep, r = divmod(v - 1, EPOCH)
                        eng.wait_ge(sems[("eng", k[1], ep)], r + 1)
                ins = o.fn(eng)
                if o.dma_key is not None:
                    ins.then_inc(sems[("dma", o.dma_key)], 16)
                elif o.signal:
                    ep, r = divmod(o.sigval - 1, EPOCH)
                    ins.then_inc(sems[("eng", e, ep)], 1)

        @block.tensor
        def _(eng):
            run("pe", eng)

        @block.scalar
        def _(eng):
            run("act", eng)

        @block.vector
        def _(eng):
            run("dve", eng)

        @block.gpsimd
        def _(eng):
            run("pool", eng)

        @block.sync
        def _(eng):
            run("sp", eng)
            for k, v in self.dma_keys.items():
                eng.wait_ge(sems[("dma", k)], v)


class T:
    __slots__ = ("ap", "res")

    def __init__(self, ap, res):
        self.ap = ap
        self.res = res

    def __getitem__(self, k):
        return T(self.ap[k], self.res)

    def r(self, s, **kw):
        return T(self.ap.rearrange(s, **kw), self.res)

    def bc(self, shape):
        return T(self.ap.to_broadcast(shape), self.res)

    def us(self, ax):
        return T(self.ap.unsqueeze(ax), self.res)


def _res(*xs):
    out = []
    for x in xs:
        if isinstance(x, T):
            out.append(x.res)
    return out


def _ap(x):
    return x.ap if isinstance(x, T) else x


class K:
    def __init__(self, NCH, DEPTH):
        self.NCH = NCH
        self.DEPTH = DEPTH
        self.L = NCH * 128
        self.S_ = (NCH - 1) * 128
        nc = self.nc = bass.Bass("TRN2", target_bir_lowering=False)
        self.S = Sched(nc)
        dt = nc.dram_tensor
        self.x = dt("x", [self.S_, D], F32, kind="ExternalInput").ap()
        self.meta = dt("meta", [16, D], F32, kind="ExternalInput").ap()
        self.w_in = dt("w_in", [2, D, DIN], F32, kind="ExternalInput").ap()
        self.w_out = dt("w_out", [2, D, D], F32, kind="ExternalInput").ap()
        self.w_up = dt("w_up", [2, D, DFF], F32, kind="ExternalInput").ap()
        self.w_down = dt("w_down", [2, DFF, D], F32, kind="ExternalInput").ap()
        self.pk_d = dt("pk", [128, 2, 56], F32, kind="ExternalInput").ap()
        self.pq_d = dt("pq", [64, 2, 8], F32, kind="ExternalInput").ap()
        self.pq2_d = dt("pq2", [128, 2, 2], F32, kind="ExternalInput").ap()
        self.bcp_d = dt("bcp", [128, 2, 1304], F32, kind="ExternalInput").ap()
        self.out = dt("out", [self.S_, D], F32, kind="ExternalOutput").ap()
        self.hbuf = dt("hbuf", [self.L, D], F32, kind="Internal").ap()
        self.dmak = 0
        import os
        self.stop = float(os.environ.get("KSTOP", "9"))

    def act(self, out, in_, func, bias=None, scale=1.0, accum=None, eng="act"):
        kw = {}
        if bias is not None:
            kw["bias"] = _ap(bias)
        if accum is not None:
            kw["accum_out"] = _ap(accum)
        o, i, s = _ap(out), _ap(in_), _ap(scale)
        self.S.op("act", lambda e: e.activation(out=o, in_=i, func=func, scale=s, **kw),
                  reads=_res(in_, bias, scale), writes=_res(out, accum))

    def ts(self, eng, out, in0, s1, s2, op0, op1=None):
        o, i, a, b = _ap(out), _ap(in0), _ap(s1), _ap(s2)
        if op1 is None:
            self.S.op(eng, lambda e: e.tensor_scalar(out=o, in0=i, scalar1=a, scalar2=None, op0=op0),
                      reads=_res(in0, s1), writes=_res(out))
        else:
            self.S.op(eng, lambda e: e.tensor_scalar(out=o, in0=i, scalar1=a, scalar2=b, op0=op0, op1=op1),
                      reads=_res(in0, s1, s2), writes=_res(out))

    def tt(self, eng, out, in0, in1, op):
        o, a, b = _ap(out), _ap(in0), _ap(in1)
        self.S.op(eng, lambda e: e.tensor_tensor(out=o, in0=a, in1=b, op=op), reads=_res(in0, in1), writes=_res(out))

    def stt(self, out, in0, scalar, in1, op0, op1):
        o, a, s, b = _ap(out), _ap(in0), _ap(scalar), _ap(in1)
        self.S.op("dve", lambda e: e.scalar_tensor_tensor(out=o, in0=a, scalar=s, in1=b, op0=op0, op1=op1),
                  reads=_res(in0, scalar, in1), writes=_res(out))

    def cp(self, eng, out, in_):
        o, i = _ap(out), _ap(in_)
        if eng == "act":
            self.S.op("act", lambda e: e.activation(out=o, in_=i, func=AF.Copy), reads=_res(in_), writes=_res(out))
        else:
            self.S.op(eng, lambda e: e.tensor_copy(out=o, in_=i), reads=_res(in_), writes=_res(out))

    def memset(self, eng, out, val):
        o = _ap(out)
        self.S.op(eng, lambda e: e.memset(o, val), writes=_res(out))

    def recip(self, out, in_):
        o, i = _ap(out), _ap(in_)
        self.S.op("dve", lambda e: e.reciprocal(out=o, in_=i), reads=_res(in_), writes=_res(out))

    def rsum(self, out, in_):
        o, i = _ap(out), _ap(in_)
        self.S.op("dve", lambda e: e.reduce_sum(out=o, in_=i, axis=AX.X), reads=_res(in_), writes=_res(out))

    def mm(self, out, lhsT, rhs, start, stop, skip=False, tp=None):
        o, l, r = _ap(out), _ap(lhsT), _ap(rhs)
        if tp is not None:
            self.S.op("pe", lambda e: e.matmul(o, lhsT=l, rhs=r, start=start, stop=stop, skip_group_check=True,
                                                 tile_position=tp), reads=_res(lhsT, rhs), writes=_res(out))
            return
        if skip:
            self.S.op("pe", lambda e: e.matmul(o, lhsT=l, rhs=r, start=start, stop=stop, skip_group_check=True),
                      reads=_res(lhsT, rhs), writes=_res(out))
        else:
            self.S.op("pe", lambda e: e.matmul(o, lhsT=l, rhs=r, start=start, stop=stop),
                      reads=_res(lhsT, rhs), writes=_res(out))

    def sigmoid(self, out, in_):
        P = out.ap.shape[0]
        self.act(out, in_, AF.Exp, scale=-1.0)
        self.act(out, out, AF.Ln, bias=self.one_c[0:P, :])
        self.act(out, out, AF.Exp, scale=-1.0)

    def tr(self, out, in_, ident):
        o, i, d = _ap(out), _ap(in_), _ap(ident)
        self.S.op("pe", lambda e: e.transpose(out=o, in_=i, identity=d), reads=_res(in_, ident), writes=_res(out))

    def dma(self, out, in_, key=None, eng="sp"):
        o, i = _ap(out), _ap(in_)
        if key is None:
            key = (out.res if isinstance(out, T) else in_.res)
        key = "k_" + str(key)
        self.S.op(eng, lambda e: e.dma_start(out=o, in_=i), reads=_res(in_), writes=_res(out), dma_key=key)

    def alloc_all(self, st):
        nc = self.nc
        self.arena_w = st.enter_context(nc.sbuf_tensor("arena_w", [128, 32768], F32))
        self.arena_k = st.enter_context(nc.sbuf_tensor("arena_k", [128, 20440], F32))
        self.psum = [st.enter_context(nc.psum_tensor(f"bank{i}", [128, 512], F32)) for i in range(8)]
        self.woff = 0
        self.koff = 0
        self.uid = 0
        self.where = {}

    def carve(self, arena, off, name, shape, dtype, parts=128):
        n = int(np.prod(shape[1:]))
        words = (n + 1) // 2 if dtype == BF16 else n
        ap = arena[0:parts, off:off + words]
        if dtype == BF16:
            ap = ap.bitcast(BF16)[:, 0:n]
        elif dtype == I32:
            ap = ap.bitcast(I32)
        if len(shape) == 3:
            ap = ap.rearrange("p (a b) -> p a b", a=shape[1])
        self.uid += 1
        t = T(ap, f"{name}#{self.uid}")
        self.where[t.res] = (arena, off, words)
        return t, words

    def alias(self, base, name, shape, dtype, parts=128, woff=0, res=None):
        arena, off, words = self.where[base.res]
        t, w = self.carve(arena, off + woff, name, shape, dtype, parts)
        assert woff + w <= words, (name, woff, w, words)
        del self.where[t.res]
        return T(t.ap, res if res is not None else base.res)

    def aw(self, name, shape, dtype, parts=128):
        t, w = self.carve(self.arena_w, self.woff, name, shape, dtype, parts)
        self.woff += w
        assert self.woff <= 32768, (name, self.woff)
        return t

    def ak(self, name, shape, dtype, parts=128):
        t, w = self.carve(self.arena_k, self.koff, name, shape, dtype, parts)
        self.koff += w
        assert self.koff <= 20440, (name, self.koff)
        return t

    def bank(self, i, shape, dtype=F32, parts=128):
        ap = self.psum[i][0:parts, :]
        n = int(np.prod(shape[1:]))
        if dtype == BF16:
            ap = ap.bitcast(BF16)[:, 0:n]
        else:
            ap = ap[:, 0:n]
        if len(shape) == 3:
            ap = ap.rearrange("p (a b) -> p a b", a=shape[1])
        return T(ap, f"bank{i}")

    def consts(self):
        ak = self.ak
        self.ident_f = ak("ident_f", [128, 128], F32)
        self.ident_b = ak("ident_b", [128, 128], BF16)
        self.tri_incl_f = ak("tri_incl_f", [128, 128], F32)
        self.low_strict_f = ak("low_strict_f", [128, 128], F32)
        self.ones_f = ak("ones_f", [128, 128], F32)
        self.mask_gt_b = ak("mask_gt_b", [128, 128], BF16)
        self.negU_b = ak("negU_b", [128, 128], BF16)
        self.negL_b = ak("negL_b", [128, 128], BF16)
        self.bd64_b = ak("bd64_b", [128, 128], BF16)
        self.ones_b = ak("ones_b", [128, 128], BF16)
        self.eps_c = ak("eps_c", [128, 1], F32)
        self.one_c = ak("one_c", [128, 1], F32)
        tmp = ak("ctmp", [128, 128], F32)
        S = self.S

        def sel(out, pattern, cm, op, base=0):
            o = out.ap
            S.op("pool", lambda e: e.affine_select(out=o, in_=o, pattern=pattern, compare_op=op, fill=0.0,
                                                   base=base, channel_multiplier=cm), reads=[out.res], writes=[out.res])
        self.memset("pool", self.ident_f, 1.0)
        sel(self.ident_f, [[1, 128]], -1, ALU.is_equal)
        self.cp("dve", self.ident_b, self.ident_f)
        self.memset("pool", self.tri_incl_f, 1.0)
        sel(self.tri_incl_f, [[1, 128]], -1, ALU.is_ge)
        self.mask_ge = self.tri_incl_f
        self.memset("pool", self.low_strict_f, 1.0)
        sel(self.low_strict_f, [[-1, 128]], 1, ALU.is_gt)
        self.memset("pool", self.ones_f, 1.0)
        self.memset("pool", tmp, 1.0)
        sel(tmp, [[1, 128]], -1, ALU.is_gt)
        self.cp("dve", self.mask_gt_b, tmp)
        self.memset("pool", tmp, -1.0)
        sel(tmp, [[-1, 128]], 1, ALU.is_ge)
        self.cp("dve", self.negU_b, tmp)
        self.memset("pool", tmp, -1.0)
        sel(tmp, [[1, 128]], -1, ALU.is_gt)
        self.cp("dve", self.negL_b, tmp)
        self.memset("pool", self.bd64_b, 1.0 / 64)
        self.memset("pool", self.bd64_b[0:64, 64:128], 0.0)
        self.memset("pool", self.bd64_b[64:128, 0:64], 0.0)
        self.memset("pool", self.ones_b, 1.0)
        self.memset("pool", self.eps_c, EPS)
        self.memset("pool", self.one_c, 1.0)
        self.pk = ak("pk", [128, 2, 56], F32)
        self.pq = ak("pq", [64, 2, 8], F32, parts=64)
        self.bcp = ak("bcp", [128, 1304], F32)
        self.dma(self.pk, self.pk_d)
        self.dma(self.pq, self.pq_d)
        self.pq2 = ak("pq2", [128, 2, 2], F32)
        self.dma(self.pq2, self.pq2_d)
        self.kbase = self.koff

    def rstd(self, out, ss, n, width):
        P = out.ap.shape[0]
        self.act(out, ss, AF.Ln, bias=self.eps_c[0:P, :], scale=1.0 / n)
        self.act(out, out, AF.Exp, scale=-0.5)

    def load_cast(self, dst, src_ap, scale, i, parts=128, width=2048):
        ns_ = len(self.stg)
        stg = self.stg[i % ns_][0:parts, 0:width]
        self.dma(stg, src_ap, key=f"stg{i % ns_}", eng=("sp" if (ns_ == 2 or i % 2 == 0) else "act"))
        eng = ("dve", "act", "pool")[i % 3] if False else ("dve", "act")[i % 2]
        if scale is None:
            self.cp(eng, dst, stg)
        elif eng == "act":
            self.act(dst, stg, AF.Copy, scale=scale)
        else:
            self.ts(eng, dst, stg, scale, None, ALU.mult)

    def build(self):
        with ExitStack() as st:
            self.alloc_all(st)
            self.consts()
            for l in range(self.DEPTH):
                self.mixer_phase(l)
                self.S.barrier()
                if self.stop >= 7:
                    self.mlp_phase(l)
                self.S.barrier()
            self.S.emit(st)
        return self.nc

    def mixer_phase(self, l):
        NCH, L = self.NCH, self.L
        self.woff = 0
        self.koff = self.kbase
        aw, ak = self.aw, self.ak
        pk, pq = self.pk, self.pq
        self.dma(self.bcp, self.bcp_d[:, l, :])
        bcp = self.bcp.r("p (o n) -> p o n", o=1)
        bcp = T(bcp.ap.to_broadcast([128, 2, 1304]) if False else bcp.ap, bcp.res)
        win = aw("win", [128, 8, DIN], BF16)
        wo = aw("wo", [128, 8, 1024], BF16)
        kT = aw("kT", [128, 2, L], BF16)
        vv = aw("vv", [128, NCH, 256], BF16)
        U = aw("U", [128, 8, 131], BF16)
        qg = aw("qg", [128, 512], F32)
        W = {}
        W["acc"] = ak("acc", [128, 8, 128], F32)
        W["ex"] = ak("ex", [128, 8, 128], F32)
        self.stg = [self.alias(W["acc"], "stg0", [128, 1024], F32), self.alias(W["ex"], "stg1", [128, 1024], F32)]
        i = 0
        for k in range(8):
            for q4 in range(4):
                c0 = q4 * 834
                self.load_cast(win[:, k, c0:c0 + 834], self.w_in[l, k * 128:(k + 1) * 128, c0:c0 + 834],
                               pk[:, l, k:k + 1], i, width=834)
                i += 1
        for k in range(8):
            self.load_cast(wo[:, k, :], self.w_out[l, k * 128:(k + 1) * 128, :], None, i, width=1024)
            i += 1
        self.ts("dve", qg[:, 0:256], bcp[:, 0, 0:256], 0.125, None, ALU.mult)
        self.cp("dve", qg[:, 256:512], bcp[:, 0, 256:512])
        dtb = bcp[:, 0, 512:520]
        aneg = ak("aneg", [128, 8], F32)
        self.act(aneg, bcp[:, 0, 520:528], AF.Exp)
        self.ts("dve", aneg, aneg, -1.0, None, ALU.mult)
        Dsk = bcp[:, 0, 528:536]
        snw = bcp[:, 0, 536:1048]
        hnw = bcp[:, 0, 1048:1304]
        cw = pk[:, l, 16:48].r("p (k t) -> p k t", t=4)
        cb = pk[:, l, 48:56]
        Wd = ak("Wd", [128, 32, 128], BF16)
        Bd = ak("Bd", [128, 8, 128], BF16)
        for kc in range(8):
            for t in range(4):
                self.ts("dve", Wd[:, kc * 4 + t, :], self.ident_f, cw[:, kc, t:t + 1], None, ALU.mult)
            self.ts("dve", Bd[:, kc, :], self.ident_f, cb[:, kc:kc + 1], None, ALU.mult)
        lb = ak("lb", [64, 4], F32, parts=64)
        oml = ak("oml", [64, 4], F32, parts=64)
        if l == 0:
            self.memset("pool", lb, 0.0)
            self.memset("pool", oml, 1.0)
        else:
            e01 = ak("e01", [64, 8], F32, parts=64)
            self.act(e01[:, 0:4], pq[:, 0, 4:8], AF.Exp)
            self.act(e01[:, 4:8], pq[:, 1, 4:8], AF.Exp)
            den = ak("den", [64, 4], F32, parts=64)
            self.tt("dve", den, e01[:, 0:4], e01[:, 4:8], ALU.add)
            self.recip(den, den)
            self.tt("dve", lb, e01[:, 0:4], den, ALU.mult)
            self.ts("dve", oml, lb, -1.0, 1.0, ALU.mult, ALU.add)
        sbw = self.pq2[:, l, :]
        self.memset("pool", U, 0.0)
        stT = ak("stT", [128, 512], F32)
        stTb = ak("stTb", [128, 512], BF16)
        self.memset("pool", stT, 0.0)
        self.memset("pool", stTb, 0.0)
        Sh = ak("Sh", [64, 256], F32, parts=64)
        Shb = ak("Shb", [64, 256], BF16, parts=64)
        self.memset("pool", Sh, 0.0)
        self.memset("pool", Shb, 0.0)
        scTm = ak("scTm", [128, 4, 128], BF16)
        self.memset("pool", scTm, 0.0)
        kpad = ak("kpad", [64, 4, 128], BF16, parts=64)
        self.memset("pool", kpad, 0.0)
        hT4 = ak("h1", [128, 1, 1024], F32)
        qT = ak("qT", [128, 2, 512], BF16)
        nqT = ak("nqT", [128, 2, 512], BF16)
        ccT = ak("ccT", [128, 6, 512], BF16)
        oTn = ak("oTn", [128, 2, 512], BF16)
        W["ez"] = ak("ez", [128, 512], F32)
        W["sq"] = self.alias(W["ez"], "sq", [128, 1024], BF16)
        W["ss"] = ak("ss", [128, 1], F32)
        W["rs"] = ak("rs", [128, 1], F32)
        W["hn"] = ak("hn", [128, 1024], BF16)
        W["hnT"] = ak("hnT", [128, 8, 128], BF16)
        W["qk"] = ak("qk", [128, 512], F32)
        W["qk2"] = self.alias(W["ez"], "qk2", [128, 512], F32)
        W["qss"] = ak("qss", [128, 8], F32)
        W["qkn"] = self.alias(W["hn"], "qkn", [128, 512], BF16)
        W["BCb"] = ak("BCb", [128, 4, 128], BF16)
        W["xtm"] = ak("xtm", [128, 512], F32)
        W["Btm"] = ak("Btm", [128, 256], BF16)
        W["dt"] = ak("dt", [128, 8], F32)
        W["a"] = ak("a", [128, 8], F32)
        W["sm"] = ak("sm", [128, 32], F32)
        W["rhsA"] = self.alias(W["ex"], "rhsA", [128, 8, 128], F32)
        W["decT"] = self.alias(W["acc"], "decT", [128, 8, 128], F32)
        W["cbm"] = self.alias(W["hn"], "cbm", [128, 2, 128], F32, woff=256)
        W["MT"] = self.alias(W["qk"], "MT", [128, 8, 128], BF16)
        W["xdt"] = ak("xdt", [128, 512], BF16)
        W["xw"] = ak("xw", [128, 512], BF16)
        W["yoff"] = ak("yoff", [128, 512], F32)
        W["y"] = ak("y", [128, 512], F32)
        W["cc"] = ak("cc", [128, 768], BF16)
        W["t3"] = ak("t3", [64, 4, 1], F32, parts=64)
        W["fm"] = [self.alias(W["acc"], "fm0", [64, 4, 128], F32, parts=64),
                   self.alias(W["acc"], "fm1", [64, 4, 128], F32, parts=64, woff=512),
                   self.alias(W["ex"], "fm2", [64, 4, 128], F32, parts=64),
                   self.alias(W["ex"], "fm3", [64, 4, 128], F32, parts=64, woff=512),
                   self.alias(W["yoff"], "fm4", [64, 4, 128], F32, parts=64),
                   self.alias(W["xtm"], "fm5", [64, 4, 128], F32, parts=64),
                   W["t3"]]
        W["fmb"] = [ak("fmb0", [64, 4, 128], BF16, parts=64), ak("fmb1", [64, 4, 128], BF16, parts=64),
                    self.alias(W["xdt"], "fmb2", [64, 4, 128], BF16, parts=64),
                    self.alias(W["xw"], "fmb3", [64, 4, 128], BF16, parts=64)]
        W["vtm"] = ak("vtm", [128, 256], BF16)
        W["ketm"] = ak("ketm", [128, 256], BF16)
        W["o"] = self.alias(W["y"], "o", [128, 256], F32)
        W["eg"] = self.alias(W["y"], "eg", [128, 256], F32, woff=256)
        W["o2"] = self.alias(W["ez"], "o2", [128, 256], F32)
        W["oss"] = ak("oss", [128, 8], F32)
        AT = []
        ebase = (W["acc"], W["ex"])
        spb = (W["qk"], W["hn"])
        wb = (W["xtm"], W["yoff"])
        ob = (W["y"], W["ez"])
        o2b = (W["xdt"], W["xw"])
        for s in range(2):
            AT.append(dict(e=self.alias(ebase[s], f"ae{s}", [128, 512], F32),
                           sp=self.alias(spb[s], f"asp{s}", [128, 512], BF16),
                           w=self.alias(wb[s], f"aw{s}", [128, 512], BF16),
                           o=self.alias(ob[s], f"ao{s}", [64, 512], F32, parts=64),
                           o2=self.alias(o2b[s], f"ao2{s}", [64, 512], BF16, parts=64),
                           r=self.alias(ebase[s], f"ar{s}", [64, 512], F32, parts=64, woff=512)))
        for s_ in range(2):
            AT[s_]["z"] = aw(f"az{s_}", [128, 512], F32)
        PT = dict(o=self.alias(W["acc"], "po_", [128, 512], F32), o2=self.alias(W["xdt"], "po2_", [128, 512], BF16),
                  r=self.alias(W["ex"], "pr_", [128, 512], F32))
        hres = self.alias(W["acc"], "hres", [128, 1024], F32)
        hmid = hres

        ctx = dict(Wd=Wd, Bd=Bd, l=l, win=win, wo=wo, kT=kT, vv=vv, qg=qg, dtb=dtb, aneg=aneg, Dsk=Dsk, snw=snw,
                   hnw=hnw, cw=cw, cb=cb, lb=lb, oml=oml, sbw=sbw, U=U, stT=stT, stTb=stTb, Sh=Sh, Shb=Shb,
                   scTm=scTm, kpad=kpad, PT=PT, hres=hres, hT4=hT4, qT=qT, nqT=nqT, ccT=ccT, oTn=oTn, W=W, AT=AT, hmid=hmid)
        if self.stop < 1.05:
            return
        scs = [[0]] + [list(range(c, min(c + 4, NCH))) for c in range(1, NCH, 4)]
        for sc in scs:
            for j, c in enumerate(sc):
                self.chunk(ctx, c, j)
            if self.stop >= 5:
                self.attention(ctx, sc)
            if self.stop >= 6:
                for j, c in enumerate(sc):
                    self.outproj(ctx, c, j)

    def chunk(self, X, c, j):
        l = X["l"]
        W = X["W"]
        win = X["win"]
        h = X["hT4"][:, 0, :]
        if l == 0:
            if c == 0:
                self.memset("pool", h, 0.0)
                self.dma(h[PADN:128, :], self.meta, key="h0")
            else:
                self.dma(h, self.x[(c - 1) * 128:c * 128, :], key="h0")
        else:
            self.dma(h, self.hbuf[c * 128:(c + 1) * 128, :], key="h0")
        self.act(W["sq"], h, AF.Square, accum=W["ss"])
        self.rstd(W["rs"], W["ss"], 1024, 1)
        self.ts("dve", W["hn"], h, W["rs"], None, ALU.mult)
        if self.stop < 1.15:
            return
        ptr = self.bank(7, [128, 8, 128], BF16)
        for k in range(8):
            self.tr(ptr[:, k, :], W["hn"][:, k * 128:(k + 1) * 128], self.ident_b)
        self.cp("dve", W["hnT"], ptr)
        hnT = W["hnT"]
        if self.stop < 1.25:
            return

        def proj_tm(bank, c0, n):
            ps = self.bank(bank, [128, n])
            for k in range(8):
                self.mm(ps, hnT[:, k, :], win[:, k, c0:c0 + n], k == 0, k == 7)
            return ps

        def proj_fm(bank, col0, ncol, slot):
            ps = self.bank(bank, [128, 4, 128])[0:ncol, slot, :]
            for k in range(8):
                self.mm(ps, win[:, k, col0:col0 + ncol], hnT[:, k, :], k == 0, k == 7)
            return ps

        pqk = proj_tm(0, 0, 512)
        self.cp("act", W["qk"], pqk)
        self.tt("dve", W["qk2"], W["qk"], W["qk"], ALU.mult)
        self.rsum(W["qss"], W["qk2"].r("p (a b) -> p a b", b=64))
        self.rstd(W["qss"], W["qss"], 64, 8)
        self.tt("dve", W["qk2"].r("p (a b) -> p a b", b=64), W["qk"].r("p (a b) -> p a b", b=64),
                W["qss"].us(2).bc([128, 8, 64]), ALU.mult)
        self.tt("dve", W["qkn"], W["qk2"], X["qg"], ALU.mult)
        if self.stop < 1.35:
            return
        ptq = self.bank(6, [128, 4, 128], BF16)
        for i in range(4):
            self.tr(ptq[:, i, :], W["qkn"][:, i * 128:(i + 1) * 128], self.ident_b)
        if self.stop < 1.365:
            return
        self.cp("dve", X["qT"][:, :, j * 128:(j + 1) * 128], ptq[:, 0:2, :])
        if self.stop < 1.375:
            return
        self.ts("dve", X["nqT"][:, :, j * 128:(j + 1) * 128], ptq[:, 0:2, :], -1.0, None, ALU.mult)
        if self.stop < 1.385:
            return
        self.cp("dve", X["kT"][:, :, c * 128:(c + 1) * 128], ptq[:, 2:4, :])
        if self.stop < 1.45:
            return
        pv = proj_tm(1, 512, 256)
        self.cp("act", X["vv"][:, c, :], pv)

        if self.stop < 3:
            return
        U = X["U"]
        for g in range(2):
            for s in range(4):
                proj_fm(2 + g, 1280 + (g * 4 + s) * 128, 128, s)
            self.cp("act" if g == 0 else "dve", U[:, g * 4:(g + 1) * 4, 3:131], self.bank(2 + g, [128, 4, 128]))
        if c == 0:
            self.memset("pool", U[:, :, 3:3 + PADN], 0.0)
        Wd, Bd = X["Wd"], X["Bd"]
        accp = [self.bank(4, [128, 4, 128]), self.bank(5, [128, 4, 128])]
        for g in range(2):
            for s_ in range(4):
                kc = g * 4 + s_
                o_ = accp[g][:, s_, :]
                for t in range(4):
                    self.mm(o_, Wd[:, kc * 4 + t, :], U[:, kc, t:t + 128], t == 0, False)
                self.mm(o_, Bd[:, kc, :], self.ones_b, False, True)
        self.cp("dve", U[:, :, 0:3], U[:, :, 128:131])
        ex = W["ex"]
        for g in range(2):
            self.sigmoid(ex[:, g * 4:(g + 1) * 4, :], accp[g])
        self.tt("dve", ex[:, 0:4, :], accp[0], ex[:, 0:4, :], ALU.mult)
        self.tt("dve", W["BCb"], accp[1], ex[:, 4:8, :], ALU.mult)
        BCb = W["BCb"]
        pxt = self.bank(2, [128, 512])
        for k in range(4):
            self.tr(pxt[:, k * 128:(k + 1) * 128], ex[:, k, :], self.ident_f)
        self.cp("act", W["xtm"], pxt)
        pbt = self.bank(3, [128, 256], BF16)
        for g in range(2):
            self.tr(pbt[:, g * 128:(g + 1) * 128], BCb[:, g, :], self.ident_b)
        self.cp("dve", W["Btm"], pbt)
        pdt = proj_tm(1, 2304, 8)
        dt_, a_ = W["dt"], W["a"]
        self.tt("dve", dt_, pdt, X["dtb"], ALU.add)
        self.act(dt_, dt_, AF.Exp)
        self.act(dt_, dt_, AF.Ln, bias=self.one_c)
        if c == 0:
            self.memset("pool", dt_[0:PADN, :], 0.0)
        self.tt("dve", a_, dt_, X["aneg"], ALU.mult)
        psm = self.bank(1, [128, 16])
        self.mm(psm[:, 0:8], self.tri_incl_f, a_, True, True)
        self.mm(psm[:, 8:16], self.ones_f, a_, True, True)
        sm = W["sm"]
        self.act(sm[:, 0:8], psm[:, 0:8], AF.Exp)
        self.act(sm[:, 8:16], psm[:, 8:16], AF.Exp)
        self.tt("dve", sm[:, 16:24], psm[:, 8:16], psm[:, 0:8], ALU.subtract) if False else None
        self.cp("dve", sm[:, 24:32], psm[:, 0:8])
        self.tt("dve", sm[:, 16:24], psm[:, 8:16], sm[:, 24:32], ALU.subtract)
        self.act(sm[:, 16:24], sm[:, 16:24], AF.Exp)
        self.tt("dve", sm[:, 24:32], sm[:, 16:24], dt_, ALU.mult)
        xtm3 = W["xtm"].r("p (a b) -> p a b", b=64)
        self.tt("dve", W["xdt"].r("p (a b) -> p a b", b=64), xtm3, dt_.us(2).bc([128, 8, 64]), ALU.mult)
        self.tt("dve", W["xw"].r("p (a b) -> p a b", b=64), xtm3, sm[:, 24:32].us(2).bc([128, 8, 64]), ALU.mult)
        rhsA = W["rhsA"]
        self.tt("dve", rhsA, self.tri_incl_f.us(1).bc([128, 8, 128]), a_.us(2).bc([128, 8, 128]), ALU.mult)
        for half in range(2):
            pseg = self.bank(2 + half, [128, 4, 128])
            self.mm(pseg, self.low_strict_f, rhsA[:, half * 4:(half + 1) * 4, :], True, True)
            self.act(W["decT"][:, half * 4:(half + 1) * 4, :], pseg, AF.Exp)
        pcb = self.bank(5, [128, 2, 128])
        for g in range(2):
            self.mm(pcb[:, g, :], BCb[:, g, :], BCb[:, 2 + g, :], True, True)
        self.tt("dve", W["cbm"], pcb, self.mask_ge.us(1).bc([128, 2, 128]), ALU.mult)
        for g in range(2):
            self.tt("dve", W["MT"][:, g * 4:(g + 1) * 4, :], W["decT"][:, g * 4:(g + 1) * 4, :],
                    W["cbm"][:, g:g + 1, :].bc([128, 4, 128]), ALU.mult)
        pyd = self.bank(4, [128, 512])
        pyo = self.bank(0, [128, 512])
        for hh in range(8):
            self.mm(pyd[:, hh * 64:(hh + 1) * 64], W["MT"][:, hh, :], W["xdt"][:, hh * 64:(hh + 1) * 64], True, True)
        for g in range(2):
            self.mm(pyo[:, g * 256:(g + 1) * 256], BCb[:, 2 + g, :], X["stTb"][:, g * 256:(g + 1) * 256], True, True)
        pst = self.bank(1, [128, 512])
        for g in range(2):
            self.mm(pst[:, g * 256:(g + 1) * 256], W["Btm"][:, g * 128:(g + 1) * 128], W["xw"][:, g * 256:(g + 1) * 256], True, True)
        y = W["y"]
        stT = X["stT"]
        v3 = "p (a b) -> p a b"
        self.tt("dve", y.r(v3, b=64), pyo.r(v3, b=64), sm[:, 0:8].us(2).bc([128, 8, 64]), ALU.mult)
        self.tt("dve", y, y, pyd, ALU.add)
        self.tt("dve", stT.r(v3, b=64), stT.r(v3, b=64), sm[:, 8:16].us(2).bc([128, 8, 64]), ALU.mult)
        self.tt("dve", stT, stT, pst, ALU.add)
        self.cp("act", X["stTb"], stT)
        self.tt("dve", W["yoff"].r("p (a b) -> p a b", b=64), xtm3, X["Dsk"].us(2).bc([128, 8, 64]), ALU.mult)
        self.tt("dve", y, y, W["yoff"], ALU.add)
        pz = proj_tm(3, 768, 512)
        ez = W["ez"]
        self.sigmoid(ez, pz)
        self.tt("dve", ez, ez, pz, ALU.mult)
        self.tt("dve", y, y, ez, ALU.mult)
        self.tt("dve", ez, y, y, ALU.mult)
        self.rsum(W["oss"][:, 0:2], ez.r("p (a b) -> p a b", b=256))
        self.rstd(W["oss"][:, 0:2], W["oss"][:, 0:2], 256, 2)
        self.tt("dve", y.r("p (a b) -> p a b", b=256), y.r("p (a b) -> p a b", b=256),
                W["oss"][:, 0:2].us(2).bc([128, 2, 256]), ALU.mult)
        self.tt("dve", W["cc"][:, 0:512], y, X["snw"], ALU.mult)

        if self.stop < 4:
            return
        fm, fmb = W["fm"], W["fmb"]
        for hh in range(4):
            proj_fm(2, 2312 + hh * 64, 64, hh)
        for hh in range(4):
            proj_fm(3, 2568 + hh * 64, 64, hh)
        pq_ = T(self.bank(2, [128, 4, 128]).ap[0:64], "bank2")
        pf_ = T(self.bank(3, [128, 4, 128]).ap[0:64], "bank3")
        lb4 = X["lb"].us(2).bc([64, 4, 128])
        oml4 = X["oml"].us(2).bc([64, 4, 128])
        sq_, f_, g_, k_, t1, t2, t3 = fm
        self.sigmoid(t1, pq_)
        self.tt("dve", sq_, t1, pq_, ALU.mult)
        self.sigmoid(t1, pf_)
        self.tt("dve", t1, t1, oml4, ALU.mult)
        self.tt("dve", f_, t1, lb4, ALU.add)
        self.ts("dve", k_, f_, -1.0, 1.0, ALU.mult, ALU.add)
        self.ts("dve", f_, f_, TINY, None, ALU.max)
        self.act(f_, f_, AF.Ln)
        if c == 0:
            self.memset("pool", f_[:, :, 0:PADN], 0.0)
            self.memset("pool", k_[:, :, 0:PADN], 0.0)
        for hh in range(4):
            fo, fi = g_.ap[:, hh, :], f_.ap[:, hh, :]
            self.S.op("dve", lambda e, fo=fo, fi=fi: e.tensor_tensor_scan(out=fo, data0=fi, data1=fi, initial=0.0,
                                                                           op0=ALU.add, op1=ALU.bypass),
                      reads=[f_.res], writes=[g_.res])
        gref = g_[:, :, 63:64].bc([64, 4, 128])
        gtot = g_[:, :, 127:128].bc([64, 4, 128])
        self.tt("dve", t1, g_, gref, ALU.subtract)
        self.act(t2, t1, AF.Exp)
        self.tt("dve", fmb[0], sq_, t2, ALU.mult)
        self.act(t2, t1, AF.Exp, scale=-1.0)
        self.tt("dve", fmb[1], k_, t2, ALU.mult)
        self.tt("dve", X["kpad"][:, :, 64:128], k_[:, :, 64:128], t2[:, :, 64:128], ALU.mult)
        self.act(t2, g_, AF.Exp)
        self.tt("dve", fmb[2], sq_, t2, ALU.mult)
        self.tt("dve", t1, gtot, g_, ALU.subtract)
        self.act(t2, t1, AF.Exp)
        self.tt("dve", fmb[3], k_, t2, ALU.mult)
        self.act(t3[:, :, 0:1], g_[:, :, 127:128], AF.Exp)
        pke = self.bank(5, [128, 4, 64], BF16)
        for hh in range(4):
            self.tr(pke[:, hh, :], fmb[3][:, hh, :], self.ident_b[0:64, 0:64])
        self.cp("dve", W["ketm"], pke.r("p a b -> p (a b)"))
        pfi = proj_tm(0, 2824, 512)
        self.cp("dve", W["vtm"], pfi[:, 0:256])
        if c == 0:
            self.memset("pool", W["vtm"][0:PADN, :], 0.0)
        psc = self.bank(2, [128, 4, 128])
        scTm = X["scTm"]
        for hh in range(4):
            kt, qt = fmb[1][:, hh, :], fmb[0][:, hh, :]
            self.mm(psc[:, hh, 64:128], X["kpad"][:, hh, :], qt[:, 64:128], True, True)
            self.mm(psc[0:64, hh, 0:128], kt[:, 0:64], qt[:, 0:128], True, True)
        self.tt("dve", scTm[0:64, :, :], psc[0:64, :, :], self.mask_ge[0:64, :].us(1).bc([64, 4, 128]), ALU.mult)
        self.tt("dve", scTm[64:128, :, 64:128], psc[64:128, :, 64:128],
                self.mask_ge[64:128, 64:128].us(1).bc([64, 4, 64]), ALU.mult)
        po_ = self.bank(3, [128, 256])
        for hh in range(4):
            sl = slice(hh * 64, (hh + 1) * 64)
            self.mm(po_[:, sl], scTm[:, hh, :], W["vtm"][:, sl], True, False)
            self.mm(po_[:, sl], fmb[2][:, hh, :], X["Shb"][:, sl], False, True)
        pS = self.bank(1, [128, 256])
        pS = T(pS.ap[0:64], "bank1")
        for hh in range(4):
            sl = slice(hh * 64, (hh + 1) * 64)
            self.mm(pS[:, sl], W["ketm"][:, sl], W["vtm"][:, sl], True, True)
        Sh = X["Sh"]
        self.tt("dve", Sh.r("p (a b) -> p a b", b=64), Sh.r("p (a b) -> p a b", b=64),
                t3[:, :, 0:1].bc([64, 4, 64]), ALU.mult)
        self.tt("dve", Sh, Sh, pS, ALU.add)
        self.cp("act", X["Shb"], Sh)
        o, o2 = W["o"], W["o2"]
        self.cp("act", o, po_)
        self.tt("dve", o2, o, o, ALU.mult)
        self.rsum(W["oss"][:, 4:8], o2.r("p (a b) -> p a b", b=64))
        self.rstd(W["oss"][:, 4:8], W["oss"][:, 4:8], 64, 4)
        self.tt("dve", o.r("p (a b) -> p a b", b=64), o.r("p (a b) -> p a b", b=64),
                W["oss"][:, 4:8].us(2).bc([128, 4, 64]), ALU.mult)
        self.tt("dve", o, o, X["hnw"], ALU.mult)
        eg = W["eg"]
        self.sigmoid(eg, pfi[:, 256:512])
        self.tt("dve", eg, eg, pfi[:, 256:512], ALU.mult)
        self.tt("dve", W["cc"][:, 512:768], o, eg, ALU.mult)
        pct = self.bank(7, [128, 6, 128], BF16)
        for k in range(6):
            self.tr(pct[:, k, :], W["cc"][:, k * 128:(k + 1) * 128], self.ident_b)
        self.cp("dve", X["ccT"][:, :, j * 128:(j + 1) * 128], pct)

    def attention(self, X, sc):
        c0 = sc[0]
        nq = len(sc)
        N = nq * 128
        kT, vv, qT, nqT, AT = X["kT"], X["vv"], X["qT"], X["nqT"], X["AT"]
        blocks = list(range(c0 + nq - 1, -1, -1))
        for hp in range(2):
            heads = (2 * hp, 2 * hp + 1)
            pz = [self.bank(0 + s, [128, 512]) for s in range(2)]
            px = [self.bank(2 + s, [128, 512]) for s in range(2)]
            pop = self.bank(4, [128, 512])
            po = [T(pop.ap[64 * s:64 * s + 64], "bank4") for s in range(2)]

            def lo_of(sb):
                return max(0, (sb - c0) * 128)

            def kq(h, sb, lo, neg=False):
                r0 = (h % 2) * 64
                kb = kT[r0:r0 + 64, h // 2, sb * 128:(sb + 1) * 128]
                q = (nqT if neg else qT)[r0:r0 + 64, h // 2, lo:N]
                return kb, q

            def stageA(sb, s):
                lo = lo_of(sb)
                kb, q = kq(heads[s], sb, lo)
                self.mm(pz[s][:, lo:N], kb, q, True, True)

            def stageB(sb, s):
                lo = lo_of(sb)
                a = AT[s]
                self.act(a["e"][:, lo:N], pz[s][:, lo:N], AF.Exp)
                zo, zi = a["z"].ap[:, lo:N], pz[s].ap[:, lo:N]
                self.S.op("dve", lambda e, zo=zo, zi=zi: e.tensor_copy(out=zo, in_=zi),
                          reads=[pz[s].res, a["e"].res], writes=[a["z"].res])
                self.act(a["sp"][:, lo:N], a["e"][:, lo:N], AF.Ln, bias=self.one_c)
                if sb >= c0:
                    self.tt("dve", a["sp"][:, lo:lo + 128], a["sp"][:, lo:lo + 128], self.mask_gt_b, ALU.mult)
                if sb == 0:
                    self.memset("pool", a["sp"][0:PADN, lo:N], 0.0)

            def stageC1(sb, s):
                lo = lo_of(sb)
                kb, q = kq(heads[s], sb, lo)
                pass

            def stageC2(sb, s):
                lo = lo_of(sb)
                self.mm(px[s][:, lo:N], self.negU_b, AT[s]["sp"][:, lo:N], sb == blocks[0], False, skip=True)
                self.tt("dve", AT[s]["e"][:, lo:N], px[s][:, lo:N], AT[s]["z"][:, lo:N], ALU.add)

            def stageD(sb, s):
                lo = lo_of(sb)
                a = AT[s]
                self.act(a["w"][:, lo:N], a["e"][:, lo:N], AF.Exp)
                if sb >= c0:
                    self.tt("dve", a["w"][:, lo:lo + 128], a["w"][:, lo:lo + 128], self.mask_gt_b, ALU.mult)
                if sb == 0:
                    self.memset("pool", a["w"][0:PADN, lo:N], 0.0)

            def stageE1(sb, s):
                lo = lo_of(sb)
                pass

            def stageE2(sb, s):
                h = heads[s]
                lo = lo_of(sb)
                if sb > 0:
                    self.mm(px[s][:, lo:N], self.negL_b, AT[s]["sp"][:, lo:N], False, False, skip=True)
                vb = vv[:, sb, h * 64:(h + 1) * 64]
                self.mm(po[s][:, lo:N], vb, AT[s]["w"][:, lo:N], sb == blocks[0], sb == 0, skip=True, tp=(0, 64 * s))

            for s in range(2):
                stageA(blocks[0], s)
            for r, sb in enumerate(blocks):
                for s in range(2):
                    stageB(sb, s)
                for s in range(2):
                    stageC1(sb, s)
                for s in range(2):
                    stageC2(sb, s)
                if r + 1 < len(blocks):
                    for s in range(2):
                        stageA(blocks[r + 1], s)
                for s in range(2):
                    stageD(sb, s)
                for s in range(2):
                    stageE1(sb, s)
                for s in range(2):
                    stageE2(sb, s)
            P = X["PT"]
            self.cp("act", P["o"][:, 0:N], pop[:, 0:N])
            self.tt("dve", P["o2"][:, 0:N], P["o"][:, 0:N], P["o"][:, 0:N], ALU.mult)
            pms = self.bank(6, [128, 512])
            self.mm(pms[:, 0:N], self.bd64_b, P["o2"][:, 0:N], True, True)
            self.act(P["r"][:, 0:N], pms[:, 0:N], AF.Ln, bias=self.eps_c)
            self.act(P["r"][:, 0:N], P["r"][:, 0:N], AF.Exp, scale=-0.5)
            self.stt(X["oTn"][:, hp, 0:N], P["o"][:, 0:N], X["sbw"][:, hp:hp + 1], P["r"][:, 0:N], ALU.mult, ALU.mult)

    def outproj(self, X, c, j):
        h = X["hres"]
        hmid = X["hmid"]
        tsl = slice(j * 128, (j + 1) * 128)
        l = X["l"]
        if l == 0:
            if c == 0:
                self.memset("pool", h, 0.0)
                self.dma(h[PADN:128, :], self.meta, key="hres")
            else:
                self.dma(h, self.x[(c - 1) * 128:c * 128, :], key="hres")
        else:
            self.dma(h, self.hbuf[c * 128:(c + 1) * 128, :], key="hres")
        for half in range(2):
            ps = self.bank(6 + half, [128, 512])
            ns = slice(half * 512, (half + 1) * 512)
            for k in range(2):
                self.mm(ps, X["oTn"][:, k, tsl], X["wo"][:, k, ns], k == 0, False)
            for k in range(6):
                self.mm(ps, X["ccT"][:, k, tsl], X["wo"][:, 2 + k, ns], False, k == 5)
            self.tt("dve", hmid[:, ns], ps, h[:, ns], ALU.add)
        self.dma(self.hbuf[c * 128:(c + 1) * 128, :], hmid, key="hres")

    def mlp_phase(self, l):
        NCH = self.NCH
        self.woff = 0
        self.koff = self.kbase
        aw, ak = self.aw, self.ak
        wup = aw("wup", [128, 8, DFF], BF16)
        wdn = aw("wdn", [128, 32, 1024], BF16)
        aT_region = ak("maTr", [128, 8192], F32)
        self.stg = [self.alias(aT_region, f"stg{q}", [128, 2048], F32, woff=2048 * q, res=f"mstg{q}_{l}")
                    for q in range(4)]
        i = 0
        for k in range(8):
            for hlf in range(2):
                self.load_cast(wup[:, k, hlf * 2048:(hlf + 1) * 2048],
                               self.w_up[l, k * 128:(k + 1) * 128, hlf * 2048:(hlf + 1) * 2048],
                               self.pk[:, l, 8 + k:9 + k], i)
                i += 1
        wdv = self.w_down[l].rearrange("(f p) n -> p f n", p=128)
        for f2 in range(16):
            stg = self.stg[i % 4]
            self.dma(stg.r("p (a b) -> p a b", a=2), wdv[:, f2 * 2:f2 * 2 + 2, :], key=f"stg{i % 4}",
                     eng=("sp" if i % 2 == 0 else "act"))
            self.cp(("dve", "act")[i % 2], wdn[:, f2 * 2:f2 * 2 + 2, :].r("p a b -> p (a b)"), stg)
            i += 1
        self.S.barrier()
        aT = self.alias(aT_region, "maT", [128, 32, 512], BF16)
        hs = [ak(f"mh{s}", [128, 1024], F32) for s in range(2)]
        hns = [ak(f"mhn{s}", [128, 1024], BF16) for s in range(2)]
        hnT = ak("mhnT", [128, 8, 512], BF16)
        hr = [ak(f"mr{s}", [128, 1024], F32) for s in range(2)]
        rl = [ak(f"mrl{s}", [128, 512], F32) for s in range(2)]
        sq = self.alias(rl[0], "msq", [128, 1024], BF16)
        ss = ak("mss", [128, 4], F32)
        last = (l == self.DEPTH - 1)
        tiles = []
        c = 1 if last else 0
        while c < NCH:
            n = min(4, NCH - c)
            tiles.append((c, n))
            c += n
        gbc = [0]

        def front(ti):
            c, n = tiles[ti]
            for b in range(n):
                gb = gbc[0]
                h = hs[gb % 2]
                hn = hns[gb % 2]
                self.dma(h, self.hbuf[(c + b) * 128:(c + b + 1) * 128, :], key=f"mh{gb % 2}")
                self.act(sq, h, AF.Square, accum=ss[:, b:b + 1])
                self.rstd(ss[:, b:b + 1], ss[:, b:b + 1], 1024, 1)
                self.ts("dve", hn, h, ss[:, b:b + 1], None, ALU.mult)
                ptr = self.bank(6 + b % 2, [128, 8, 128], BF16)
                for k in range(8):
                    self.tr(ptr[:, k, :], hn[:, k * 128:(k + 1) * 128], self.ident_b)
                self.cp("dve", hnT[:, :, b * 128:(b + 1) * 128], ptr)
                gbc[0] += 1

        def up(ti):
            c, n = tiles[ti]
            N = n * 128
            for f in range(32):
                ps = self.bank(f % 4, [128, 512])
                for k in range(8):
                    self.mm(ps[:, 0:N], wup[:, k, f * 128:(f + 1) * 128], hnT[:, k, 0:N], k == 0, k == 7)
                r = rl[f % 2]
                self.act(r[:, 0:N], ps[:, 0:N], AF.Relu)
                self.tt("dve", aT[:, f, 0:N], r[:, 0:N], r[:, 0:N], ALU.mult)

        def down(ti):
            c, n = tiles[ti]
            for b in range(n):
                cc = c + b
                hout = hr[b % 2]
                self.dma(hout, self.hbuf[cc * 128:(cc + 1) * 128, :], key=f"mr{b % 2}")
                for half in range(2):
                    ps = self.bank(4 + half, [128, 512])
                    ns = slice(half * 512, (half + 1) * 512)
                    for f in range(32):
                        self.mm(ps, aT[:, f, b * 128:(b + 1) * 128], wdn[:, f, ns], f == 0, f == 31)
                    self.tt("dve", hout[:, ns], ps, hout[:, ns], ALU.add)
                if last:
                    self.dma(self.out[(cc - 1) * 128:cc * 128, :], hout, key=f"mr{b % 2}")
                else:
                    self.dma(self.hbuf[cc * 128:(cc + 1) * 128, :], hout, key=f"mr{b % 2}")

        front(0)
        for ti in range(len(tiles)):
            up(ti)
            if ti + 1 < len(tiles):
                front(ti + 1)
            down(ti)


def prep_params(inp, depth=2):
    f = np.float32
    pk = np.zeros((128, 2, 56), f)
    pq = np.zeros((64, 2, 8), f)
    bcp = np.zeros((2, 1304), f)
    for l in range(depth):
        pk[:, l, 0:8] = inp["norm_mix_w"][l].reshape(8, 128).T
        pk[:, l, 8:16] = inp["norm_mlp_w"][l].reshape(8, 128).T
        cw = inp["ssd_conv_w"][l]
        pk[:, l, 16:48] = cw.reshape(4, 8, 128).transpose(2, 1, 0).reshape(128, 32)
        pk[:, l, 48:56] = inp["ssd_conv_b"][l].reshape(8, 128).T
        pq[:, l, 0:4] = inp["sb_out_norm"][l].T
        pq[:, l, 4:8] = inp["hg_lb_logits"][l].reshape(4, 64).T
        bcp[l, 0:256] = np.tile(inp["sb_q_norm"][l], 4)
        bcp[l, 256:512] = np.tile(inp["sb_k_norm"][l], 4)
        bcp[l, 512:520] = inp["ssd_dt_bias"][l]
        bcp[l, 520:528] = inp["ssd_A_log"][l]
        bcp[l, 528:536] = inp["ssd_D"][l]
        bcp[l, 536:1048] = inp["ssd_norm_w"][l].reshape(-1)
        bcp[l, 1048:1304] = inp["hg_out_norm"][l].reshape(-1)
    bcp = np.ascontiguousarray(np.broadcast_to(bcp[None], (128, 2, 1304)))
    pq2 = np.zeros((128, 2, 2), f)
    for l in range(depth):
        son = inp["sb_out_norm"][l]
        pq2[:, l, :] = son.reshape(2, 128).T
    return pk, pq, bcp, pq2


_CACHE = {}


def run(inputs, ncores, nch, depth):
    key = (nch, depth)
    if key not in _CACHE:
        _CACHE[key] = K(nch, depth).build()
    nc = _CACHE[key]
    pk, pq, bcp, pq2 = prep_params(inputs, depth)
    f = np.float32
    common = dict(meta=np.ascontiguousarray(inputs["meta_tokens"], f), w_in=np.ascontiguousarray(inputs["w_in"], f),
                  w_out=np.ascontiguousarray(inputs["w_out"], f), w_up=np.ascontiguousarray(inputs["w_up"], f),
                  w_down=np.ascontiguousarray(inputs["w_down"], f), pk=pk, pq=pq, bcp=bcp, pq2=pq2)
    in_maps = []
    for b in range(ncores):
        m = dict(common)
        m["x"] = np.ascontiguousarray(inputs["x"][b], f)
        in_maps.append(m)
    res = run_bass_kernel_spmd(nc, in_maps, core_ids=list(range(ncores)))
    return np.stack([np.asarray(r["out"]) for r in res.results], axis=0).astype(np.float32)


def kernel(**inputs):
    x = inputs["x"]
    B, S, _ = x.shape
    return run(inputs, B, S // 128 + 1, 2)
```

```python
import numpy as np
from contextlib import ExitStack
import concourse.bass as bass
import concourse.mybir as mybir
from concourse.bass_utils import run_bass_kernel_spmd

F32 = mybir.dt.float32
BF16 = mybir.dt.bfloat16
I32 = mybir.dt.int32
AF = mybir.ActivationFunctionType
ALU = mybir.AluOpType
AX = mybir.AxisListType

D = 1024
DIN = 3336
DFF = 4096
EPS = 1e-6
TINY = 1e-30
PADN = 112
ENGS = ("pe", "act", "dve", "pool", "sp")


class Op:
    __slots__ = ("eng", "fn", "deps", "signal", "sigval", "dma_key", "dma_val", "waits")

    def __init__(self, eng, fn, dma_key=None):
        self.eng = eng
        self.fn = fn
        self.deps = []
        self.signal = False
        self.sigval = None
        self.dma_key = dma_key
        self.dma_val = None
        self.waits = None


class Sched:
    def __init__(self, nc):
        self.nc = nc
        self.ops = {e: [] for e in ENGS}
        self.last_w = {}
        self.readers = {}
        self.dma_keys = {}
        self.last_eng = {}
        self.last_dma = {}
        self.pending_barrier = {}

    def _same_ok(self, e):
        return e == "pe"

    def op(self, eng, fn, reads=(), writes=(), dma_key=None):
        o = Op(eng, fn, dma_key)
        deps = []
        for r in reads:
            w = self.last_w.get(r)
            if w is not None:
                deps.append(w)
        for r in writes:
            w = self.last_w.get(r)
            if w is not None:
                deps.append(w)
            deps.extend(self.readers.get(r, ()))
        if eng in self.pending_barrier:
            deps.extend(self.pending_barrier.pop(eng))
        o.deps = deps
        for r in writes:
            self.last_w[r] = o
            self.readers[r] = []
        for r in reads:
            self.readers.setdefault(r, []).append(o)
        if dma_key is not None:
            v = self.dma_keys.get(dma_key, 0) + 16
            self.dma_keys[dma_key] = v
            o.dma_val = v
            self.last_dma[dma_key] = o
        else:
            self.last_eng[eng] = o
        self.ops[eng].append(o)
        return o

    def barrier(self):
        deps = list(self.last_eng.values()) + list(self.last_dma.values())
        for e in ENGS:
            self.pending_barrier[e] = list(deps)

    def finalize(self):
        for e in ENGS:
            for o in self.ops[e]:
                for d in o.deps:
                    if d is o or d.dma_key is not None:
                        continue
                    if d.eng != e or not self._same_ok(e):
                        d.signal = True
        self.nsig = {}
        for e in ENGS:
            c = 0
            for o in self.ops[e]:
                if o.signal and o.dma_key is None:
                    c += 1
                    o.sigval = c
            self.nsig[e] = c
        for e in ENGS:
            known = {}
            for o in self.ops[e]:
                w = {}
                for d in o.deps:
                    if d is o:
                        continue
                    if d.dma_key is not None:
                        k = ("dma", d.dma_key)
                        v = d.dma_val
                    else:
                        if d.eng == e and self._same_ok(e):
                            continue
                        k = ("eng", d.eng)
                        v = d.sigval
                    if known.get(k, 0) >= v:
                        continue
                    if w.get(k, 0) < v:
                        w[k] = v
                for k, v in w.items():
                    known[k] = v
                o.waits = w

    def emit(self, stack):
        nc = self.nc
        self.finalize()
        EPOCH = 30000
        sems = {}
        for e in ENGS:
            n = max((self.nsig[e] + EPOCH - 1) // EPOCH, 1)
            for i in range(n):
                sems[("eng", e, i)] = stack.enter_context(nc.semaphore(f"s_{e}_{i}"))
        for k in self.dma_keys:
            sems[("dma", k)] = stack.enter_context(nc.semaphore(f"d_{k}"))
        block = stack.enter_context(nc.Block())

        def run(e, eng):
            for o in self.ops[e]:
                for k, v in o.waits.items():
                    if k[0] == "dma":
                        eng.wait_ge(sems[k], v)
                    else:
                        ep, r = divmod(v - 1, EPOCH)
                        eng.wait_ge(sems[("eng", k[1], ep)], r + 1)
                ins = o.fn(eng)
                if o.dma_key is not None:
                    ins.then_inc(sems[("dma", o.dma_key)], 16)
                elif o.signal:
                    ep, r = divmod(o.sigval - 1, EPOCH)
                    ins.then_inc(sems[("eng", e, ep)], 1)

        @block.tensor
        def _(eng):
            run("pe", eng)

        @block.scalar
        def _(eng):
            run("act", eng)

        @block.vector
        def _(eng):
            run("dve", eng)

        @block.gpsimd
        def _(eng):
            run("pool", eng)

        @block.sync
        def _(eng):
            run("sp", eng)
            for k, v in self.dma_keys.items():
                eng.wait_ge(sems[("dma", k)], v)


class T:
    __slots__ = ("ap", "res")

    def __init__(self, ap, res):
        self.ap = ap
        self.res = res

    def __getitem__(self, k):
        return T(self.ap[k], self.res)

    def r(self, s, **kw):
        return T(self.ap.rearrange(s, **kw), self.res)

    def bc(self, shape):
        return T(self.ap.to_broadcast(shape), self.res)

    def us(self, ax):
        return T(self.ap.unsqueeze(ax), self.res)


def _res(*xs):
    out = []
    for x in xs:
        if isinstance(x, T):
            out.append(x.res)
    return out


def _ap(x):
    return x.ap if isinstance(x, T) else x


class K:
    def __init__(self, NCH, DEPTH):
        self.NCH = NCH
        self.DEPTH = DEPTH
        self.L = NCH * 128
        self.S_ = (NCH - 1) * 128
        nc = self.nc = bass.Bass("TRN2", target_bir_lowering=False)
        self.S = Sched(nc)
        dt = nc.dram_tensor
        self.x = dt("x", [self.S_, D], F32, kind="ExternalInput").ap()
        self.meta = dt("meta", [16, D], F32, kind="ExternalInput").ap()
        self.w_in = dt("w_in", [2, D, DIN], F32, kind="ExternalInput").ap()
        self.w_out = dt("w_out", [2, D, D], F32, kind="ExternalInput").ap()
        self.w_up = dt("w_up", [2, D, DFF], F32, kind="ExternalInput").ap()
        self.w_down = dt("w_down", [2, DFF, D], F32, kind="ExternalInput").ap()
        self.pk_d = dt("pk", [128, 2, 56], F32, kind="ExternalInput").ap()
        self.pq_d = dt("pq", [64, 2, 8], F32, kind="ExternalInput").ap()
        self.pq2_d = dt("pq2", [128, 2, 2], F32, kind="ExternalInput").ap()
        self.bcp_d = dt("bcp", [128, 2, 1304], F32, kind="ExternalInput").ap()
        self.out = dt("out", [self.S_, D], F32, kind="ExternalOutput").ap()
        self.hbuf = dt("hbuf", [self.L, D], F32, kind="Internal").ap()
        self.dmak = 0
        import os
        self.stop = float(os.environ.get("KSTOP", "9"))

    def act(self, out, in_, func, bias=None, scale=1.0, accum=None, eng="act"):
        kw = {}
        if bias is not None:
            kw["bias"] = _ap(bias)
        if accum is not None:
            kw["accum_out"] = _ap(accum)
        o, i, s = _ap(out), _ap(in_), _ap(scale)
        self.S.op("act", lambda e: e.activation(out=o, in_=i, func=func, scale=s, **kw),
                  reads=_res(in_, bias, scale), writes=_res(out, accum))

    def ts(self, eng, out, in0, s1, s2, op0, op1=None):
        o, i, a, b = _ap(out), _ap(in0), _ap(s1), _ap(s2)
        if op1 is None:
            self.S.op(eng, lambda e: e.tensor_scalar(out=o, in0=i, scalar1=a, scalar2=None, op0=op0),
                      reads=_res(in0, s1), writes=_res(out))
        else:
            self.S.op(eng, lambda e: e.tensor_scalar(out=o, in0=i, scalar1=a, scalar2=b, op0=op0, op1=op1),
                      reads=_res(in0, s1, s2), writes=_res(out))

    def tt(self, eng, out, in0, in1, op):
        o, a, b = _ap(out), _ap(in0), _ap(in1)
        self.S.op(eng, lambda e: e.tensor_tensor(out=o, in0=a, in1=b, op=op), reads=_res(in0, in1), writes=_res(out))

    def stt(self, out, in0, scalar, in1, op0, op1):
        o, a, s, b = _ap(out), _ap(in0), _ap(scalar), _ap(in1)
        self.S.op("dve", lambda e: e.scalar_tensor_tensor(out=o, in0=a, scalar=s, in1=b, op0=op0, op1=op1),
                  reads=_res(in0, scalar, in1), writes=_res(out))

    def cp(self, eng, out, in_):
        o, i = _ap(out), _ap(in_)
        if eng == "act":
            self.S.op("act", lambda e: e.activation(out=o, in_=i, func=AF.Copy), reads=_res(in_), writes=_res(out))
        else:
            self.S.op(eng, lambda e: e.tensor_copy(out=o, in_=i), reads=_res(in_), writes=_res(out))

    def memset(self, eng, out, val):
        o = _ap(out)
        self.S.op(eng, lambda e: e.memset(o, val), writes=_res(out))

    def recip(self, out, in_):
        o, i = _ap(out), _ap(in_)
        self.S.op("dve", lambda e: e.reciprocal(out=o, in_=i), reads=_res(in_), writes=_res(out))

    def rsum(self, out, in_):
        o, i = _ap(out), _ap(in_)
        self.S.op("dve", lambda e: e.reduce_sum(out=o, in_=i, axis=AX.X), reads=_res(in_), writes=_res(out))

    def mm(self, out, lhsT, rhs, start, stop, skip=False, tp=None):
        o, l, r = _ap(out), _ap(lhsT), _ap(rhs)
        if tp is not None:
            self.S.op("pe", lambda e: e.matmul(o, lhsT=l, rhs=r, start=start, stop=stop, skip_group_check=True,
                                                 tile_position=tp), reads=_res(lhsT, rhs), writes=_res(out))
            return
        if skip:
            self.S.op("pe", lambda e: e.matmul(o, lhsT=l, rhs=r, start=start, stop=stop, skip_group_check=True),
                      reads=_res(lhsT, rhs), writes=_res(out))
        else:
            self.S.op("pe", lambda e: e.matmul(o, lhsT=l, rhs=r, start=start, stop=stop),
                      reads=_res(lhsT, rhs), writes=_res(out))

    def sigmoid(self, out, in_):
        P = out.ap.shape[0]
        self.act(out, in_, AF.Exp, scale=-1.0)
        self.act(out, out, AF.Ln, bias=self.one_c[0:P, :])
        self.act(out, out, AF.Exp, scale=-1.0)

    def tr(self, out, in_, ident):
        o, i, d = _ap(out), _ap(in_), _ap(ident)
        self.S.op("pe", lambda e: e.transpose(out=o, in_=i, identity=d), reads=_res(in_, ident), writes=_res(out))

    def dma(self, out, in_, key=None, eng="sp"):
        o, i = _ap(out), _ap(in_)
        if key is None:
            key = (out.res if isinstance(out, T) else in_.res)
        key = "k_" + str(key)
        self.S.op(eng, lambda e: e.dma_start(out=o, in_=i), reads=_res(in_), writes=_res(out), dma_key=key)

    def alloc_all(self, st):
        nc = self.nc
        self.arena_w = st.enter_context(nc.sbuf_tensor("arena_w", [128, 32768], F32))
        self.arena_k = st.enter_context(nc.sbuf_tensor("arena_k", [128, 20440], F32))
        self.psum = [st.enter_context(nc.psum_tensor(f"bank{i}", [128, 512], F32)) for i in range(8)]
        self.woff = 0
        self.koff = 0
        self.uid = 0
        self.where = {}

    def carve(self, arena, off, name, shape, dtype, parts=128):
        n = int(np.prod(shape[1:]))
        words = (n + 1) // 2 if dtype == BF16 else n
        ap = arena[0:parts, off:off + words]
        if dtype == BF16:
            ap = ap.bitcast(BF16)[:, 0:n]
        elif dtype == I32:
            ap = ap.bitcast(I32)
        if len(shape) == 3:
            ap = ap.rearrange("p (a b) -> p a b", a=shape[1])
        self.uid += 1
        t = T(ap, f"{name}#{self.uid}")
        self.where[t.res] = (arena, off, words)
        return t, words

    def alias(self, base, name, shape, dtype, parts=128, woff=0, res=None):
        arena, off, words = self.where[base.res]
        t, w = self.carve(arena, off + woff, name, shape, dtype, parts)
        assert woff + w <= words, (name, woff, w, words)
        del self.where[t.res]
        return T(t.ap, res if res is not None else base.res)

    def aw(self, name, shape, dtype, parts=128):
        t, w = self.carve(self.arena_w, self.woff, name, shape, dtype, parts)
        self.woff += w
        assert self.woff <= 32768, (name, self.woff)
        return t

    def ak(self, name, shape, dtype, parts=128):
        t, w = self.carve(self.arena_k, self.koff, name, shape, dtype, parts)
        self.koff += w
        assert self.koff <= 20440, (name, self.koff)
        return t

    def bank(self, i, shape, dtype=F32, parts=128):
        ap = self.psum[i][0:parts, :]
        n = int(np.prod(shape[1:]))
        if dtype == BF16:
            ap = ap.bitcast(BF16)[:, 0:n]
        else:
            ap = ap[:, 0:n]
        if len(shape) == 3:
            ap = ap.rearrange("p (a b) -> p a b", a=shape[1])
        return T(ap, f"bank{i}")

    def consts(self):
        ak = self.ak
        self.ident_f = ak("ident_f", [128, 128], F32)
        self.ident_b = ak("ident_b", [128, 128], BF16)
        self.tri_incl_f = ak("tri_incl_f", [128, 128], F32)
        self.low_strict_f = ak("low_strict_f", [128, 128], F32)
        self.ones_f = ak("ones_f", [128, 128], F32)
        self.mask_gt_b = ak("mask_gt_b", [128, 128], BF16)
        self.negU_b = ak("negU_b", [128, 128], BF16)
        self.negL_b = ak("negL_b", [128, 128], BF16)
        self.bd64_b = ak("bd64_b", [128, 128], BF16)
        self.ones_b = ak("ones_b", [128, 128], BF16)
        self.eps_c = ak("eps_c", [128, 1], F32)
        self.one_c = ak("one_c", [128, 1], F32)
        tmp = ak("ctmp", [128, 128], F32)
        S = self.S

        def sel(out, pattern, cm, op, base=0):
            o = out.ap
            S.op("pool", lambda e: e.affine_select(out=o, in_=o, pattern=pattern, compare_op=op, fill=0.0,
                                                   base=base, channel_multiplier=cm), reads=[out.res], writes=[out.res])
        self.memset("pool", self.ident_f, 1.0)
        sel(self.ident_f, [[1, 128]], -1, ALU.is_equal)
        self.cp("dve", self.ident_b, self.ident_f)
        self.memset("pool", self.tri_incl_f, 1.0)
        sel(self.tri_incl_f, [[1, 128]], -1, ALU.is_ge)
        self.mask_ge = self.tri_incl_f
        self.memset("pool", self.low_strict_f, 1.0)
        sel(self.low_strict_f, [[-1, 128]], 1, ALU.is_gt)
        self.memset("pool", self.ones_f, 1.0)
        self.memset("pool", tmp, 1.0)
        sel(tmp, [[1, 128]], -1, ALU.is_gt)
        self.cp("dve", self.mask_gt_b, tmp)
        self.memset("pool", tmp, -1.0)
        sel(tmp, [[-1, 128]], 1, ALU.is_ge)
        self.cp("dve", self.negU_b, tmp)
        self.memset("pool", tmp, -1.0)
        sel(tmp, [[1, 128]], -1, ALU.is_gt)
        self.cp("dve", self.negL_b, tmp)
        self.memset("pool", self.bd64_b, 1.0 / 64)
        self.memset("pool", self.bd64_b[0:64, 64:128], 0.0)
        self.memset("pool", self.bd64_b[64:128, 0:64], 0.0)
        self.memset("pool", self.ones_b, 1.0)
        self.memset("pool", self.eps_c, EPS)
        self.memset("pool", self.one_c, 1.0)
        self.pk = ak("pk", [128, 2, 56], F32)
        self.pq = ak("pq", [64, 2, 8], F32, parts=64)
        self.bcp = ak("bcp", [128, 1304], F32)
        self.dma(self.pk, self.pk_d)
        self.dma(self.pq, self.pq_d)
        self.pq2 = ak("pq2", [128, 2, 2], F32)
        self.dma(self.pq2, self.pq2_d)
        self.kbase = self.koff

    def rstd(self, out, ss, n, width):
        P = out.ap.shape[0]
        self.act(out, ss, AF.Ln, bias=self.eps_c[0:P, :], scale=1.0 / n)
        self.act(out, out, AF.Exp, scale=-0.5)

    def load_cast(self, dst, src_ap, scale, i, parts=128, width=2048):
        ns_ = len(self.stg)
        stg = self.stg[i % ns_][0:parts, 0:width]
        self.dma(stg, src_ap, key=f"stg{i % ns_}", eng=("sp" if (ns_ == 2 or i % 2 == 0) else "act"))
        eng = ("dve", "act", "pool")[i % 3] if False else ("dve", "act")[i % 2]
        if scale is None:
            self.cp(eng, dst, stg)
        elif eng == "act":
            self.act(dst, stg, AF.Copy, scale=scale)
        else:
            self.ts(eng, dst, stg, scale, None, ALU.mult)

    def build(self):
        with ExitStack() as st:
            self.alloc_all(st)
            self.consts()
            for l in range(self.DEPTH):
                self.mixer_phase(l)
                self.S.barrier()
                if self.stop >= 7:
                    self.mlp_phase(l)
                self.S.barrier()
            self.S.emit(st)
        return self.nc

    def mixer_phase(self, l):
        NCH, L = self.NCH, self.L
        self.woff = 0
        self.koff = self.kbase
        aw, ak = self.aw, self.ak
        pk, pq = self.pk, self.pq
        self.dma(self.bcp, self.bcp_d[:, l, :])
        bcp = self.bcp.r("p (o n) -> p o n", o=1)
        bcp = T(bcp.ap.to_broadcast([128, 2, 1304]) if False else bcp.ap, bcp.res)
        win = aw("win", [128, 8, DIN], BF16)
        wo = aw("wo", [128, 8, 1024], BF16)
        kT = aw("kT", [128, 2, L], BF16)
        vv = aw("vv", [128, NCH, 256], BF16)
        U = aw("U", [128, 8, 131], BF16)
        qg = aw("qg", [128, 512], F32)
        W = {}
        W["acc"] = ak("acc", [128, 8, 128], F32)
        W["ex"] = ak("ex", [128, 8, 128], F32)
        self.stg = [self.alias(W["acc"], "stg0", [128, 1024], F32), self.alias(W["ex"], "stg1", [128, 1024], F32),
                    aw("stg2", [128, 1024], F32), aw("stg3", [128, 1024], F32)]
        i = 0
        for k in range(8):
            for q4 in range(4):
                c0 = q4 * 834
                self.load_cast(win[:, k, c0:c0 + 834], self.w_in[l, k * 128:(k + 1) * 128, c0:c0 + 834],
                               pk[:, l, k:k + 1], i, width=834)
                i += 1
        for k in range(8):
            self.load_cast(wo[:, k, :], self.w_out[l, k * 128:(k + 1) * 128, :], None, i, width=1024)
            i += 1
        self.ts("dve", qg[:, 0:256], bcp[:, 0, 0:256], 0.125, None, ALU.mult)
        self.cp("dve", qg[:, 256:512], bcp[:, 0, 256:512])
        dtb = bcp[:, 0, 512:520]
        aneg = ak("aneg", [128, 8], F32)
        self.act(aneg, bcp[:, 0, 520:528], AF.Exp)
        self.ts("dve", aneg, aneg, -1.0, None, ALU.mult)
        Dsk = bcp[:, 0, 528:536]
        snw = bcp[:, 0, 536:1048]
        hnw = bcp[:, 0, 1048:1304]
        cw = pk[:, l, 16:48].r("p (k t) -> p k t", t=4)
        cb = pk[:, l, 48:56]
        Wd = ak("Wd", [128, 32, 128], BF16)
        Bd = ak("Bd", [128, 8, 128], BF16)
        for kc in range(8):
            for t in range(4):
                self.ts("dve", Wd[:, kc * 4 + t, :], self.ident_f, cw[:, kc, t:t + 1], None, ALU.mult)
            self.ts("dve", Bd[:, kc, :], self.ident_f, cb[:, kc:kc + 1], None, ALU.mult)
        lb = ak("lb", [64, 4], F32, parts=64)
        oml = ak("oml", [64, 4], F32, parts=64)
        if l == 0:
            self.memset("pool", lb, 0.0)
            self.memset("pool", oml, 1.0)
        else:
            e01 = ak("e01", [64, 8], F32, parts=64)
            self.act(e01[:, 0:4], pq[:, 0, 4:8], AF.Exp)
            self.act(e01[:, 4:8], pq[:, 1, 4:8], AF.Exp)
            den = ak("den", [64, 4], F32, parts=64)
            self.tt("dve", den, e01[:, 0:4], e01[:, 4:8], ALU.add)
            self.recip(den, den)
            self.tt("dve", lb, e01[:, 0:4], den, ALU.mult)
            self.ts("dve", oml, lb, -1.0, 1.0, ALU.mult, ALU.add)
        sbw = self.pq2[:, l, :]
        self.memset("pool", U, 0.0)
        stT = ak("stT", [128, 512], F32)
        stTb = ak("stTb", [128, 512], BF16)
        self.memset("pool", stT, 0.0)
        self.memset("pool", stTb, 0.0)
        Sh = ak("Sh", [64, 256], F32, parts=64)
        Shb = ak("Shb", [64, 256], BF16, parts=64)
        self.memset("pool", Sh, 0.0)
        self.memset("pool", Shb, 0.0)
        scTm = ak("scTm", [128, 4, 128], BF16)
        self.memset("pool", scTm, 0.0)
        kpad = ak("kpad", [64, 4, 128], BF16, parts=64)
        self.memset("pool", kpad, 0.0)
        hT4 = ak("h1", [128, 1, 1024], F32)
        qT = ak("qT", [128, 2, 512], BF16)
        nqT = ak("nqT", [128, 2, 512], BF16)
        ccT = ak("ccT", [128, 6, 512], BF16)
        oTn = ak("oTn", [128, 2, 512], BF16)
        W["ez"] = ak("ez", [128, 512], F32)
        W["sq"] = self.alias(W["ez"], "sq", [128, 1024], BF16)
        W["ss"] = ak("ss", [128, 1], F32)
        W["rs"] = ak("rs", [128, 1], F32)
        W["hn"] = ak("hn", [128, 1024], BF16)
        W["hnT"] = ak("hnT", [128, 8, 128], BF16)
        W["qk"] = ak("qk", [128, 512], F32)
        W["qk2"] = self.alias(W["ez"], "qk2", [128, 512], F32)
        W["qss"] = ak("qss", [128, 8], F32)
        W["qkn"] = self.alias(W["hn"], "qkn", [128, 512], BF16)
        W["BCb"] = ak("BCb", [128, 4, 128], BF16)
        W["xtm"] = ak("xtm", [128, 512], F32)
        W["Btm"] = ak("Btm", [128, 256], BF16)
        W["dt"] = ak("dt", [128, 8], F32)
        W["a"] = ak("a", [128, 8], F32)
        W["sm"] = ak("sm", [128, 32], F32)
        W["rhsA"] = self.alias(W["ex"], "rhsA", [128, 8, 128], F32)
        W["decT"] = self.alias(W["acc"], "decT", [128, 8, 128], F32)
        W["cbm"] = self.alias(W["hn"], "cbm", [128, 2, 128], F32, woff=256)
        W["MT"] = self.alias(W["qk"], "MT", [128, 8, 128], BF16)
        W["xdt"] = ak("xdt", [128, 512], BF16)
        W["xw"] = ak("xw", [128, 512], BF16)
        W["yoff"] = ak("yoff", [128, 512], F32)
        W["y"] = ak("y", [128, 512], F32)
        W["cc"] = ak("cc", [128, 768], BF16)
        W["t3"] = ak("t3", [64, 4, 1], F32, parts=64)
        W["fm"] = [self.alias(W["acc"], "fm0", [64, 4, 128], F32, parts=64),
                   self.alias(W["acc"], "fm1", [64, 4, 128], F32, parts=64, woff=512),
                   self.alias(W["ex"], "fm2", [64, 4, 128], F32, parts=64),
                   self.alias(W["ex"], "fm3", [64, 4, 128], F32, parts=64, woff=512),
                   self.alias(W["yoff"], "fm4", [64, 4, 128], F32, parts=64),
                   self.alias(W["xtm"], "fm5", [64, 4, 128], F32, parts=64),
                   W["t3"]]
        W["fmb"] = [ak("fmb0", [64, 4, 128], BF16, parts=64), ak("fmb1", [64, 4, 128], BF16, parts=64),
                    self.alias(W["xdt"], "fmb2", [64, 4, 128], BF16, parts=64),
                    self.alias(W["xw"], "fmb3", [64, 4, 128], BF16, parts=64)]
        W["vtm"] = ak("vtm", [128, 256], BF16)
        W["ketm"] = ak("ketm", [128, 256], BF16)
        W["o"] = self.alias(W["y"], "o", [128, 256], F32)
        W["eg"] = self.alias(W["y"], "eg", [128, 256], F32, woff=256)
        W["o2"] = self.alias(W["ez"], "o2", [128, 256], F32)
        W["oss"] = ak("oss", [128, 8], F32)
        AT = []
        ebase = (W["acc"], W["ex"])
        spb = (W["qk"], W["hn"])
        wb = (W["xtm"], W["yoff"])
        ob = (W["y"], W["ez"])
        o2b = (W["xdt"], W["xw"])
        for s in range(2):
            AT.append(dict(e=self.alias(ebase[s], f"ae{s}", [128, 512], F32),
                           sp=self.alias(spb[s], f"asp{s}", [128, 512], BF16),
                           w=self.alias(wb[s], f"aw{s}", [128, 512], BF16),
                           o=self.alias(ob[s], f"ao{s}", [64, 512], F32, parts=64),
                           o2=self.alias(o2b[s], f"ao2{s}", [64, 512], BF16, parts=64),
                           r=self.alias(ebase[s], f"ar{s}", [64, 512], F32, parts=64, woff=512)))
        PT = dict(o=self.alias(W["acc"], "po_", [128, 512], F32), o2=self.alias(W["xdt"], "po2_", [128, 512], BF16),
                  r=self.alias(W["ex"], "pr_", [128, 512], F32))
        hres = self.alias(W["acc"], "hres", [128, 1024], F32)
        hmid = hres

        ctx = dict(Wd=Wd, Bd=Bd, l=l, win=win, wo=wo, kT=kT, vv=vv, qg=qg, dtb=dtb, aneg=aneg, Dsk=Dsk, snw=snw,
                   hnw=hnw, cw=cw, cb=cb, lb=lb, oml=oml, sbw=sbw, U=U, stT=stT, stTb=stTb, Sh=Sh, Shb=Shb,
                   scTm=scTm, kpad=kpad, PT=PT, hres=hres, hT4=hT4, qT=qT, nqT=nqT, ccT=ccT, oTn=oTn, W=W, AT=AT, hmid=hmid)
        if self.stop < 1.05:
            return
        scs = [[0]] + [list(range(c, min(c + 4, NCH))) for c in range(1, NCH, 4)]
        for sc in scs:
            for j, c in enumerate(sc):
                self.chunk(ctx, c, j)
            if self.stop >= 5:
                self.attention(ctx, sc)
            if self.stop >= 6:
                for j, c in enumerate(sc):
                    self.outproj(ctx, c, j)

    def chunk(self, X, c, j):
        l = X["l"]
        W = X["W"]
        win = X["win"]
        h = X["hT4"][:, 0, :]
        if l == 0:
            if c == 0:
                self.memset("pool", h, 0.0)
                self.dma(h[PADN:128, :], self.meta, key="h0")
            else:
                self.dma(h, self.x[(c - 1) * 128:c * 128, :], key="h0")
        else:
            self.dma(h, self.hbuf[c * 128:(c + 1) * 128, :], key="h0")
        self.act(W["sq"], h, AF.Square, accum=W["ss"])
        self.rstd(W["rs"], W["ss"], 1024, 1)
        self.ts("dve", W["hn"], h, W["rs"], None, ALU.mult)
        if self.stop < 1.15:
            return
        ptr = self.bank(7, [128, 8, 128], BF16)
        for k in range(8):
            self.tr(ptr[:, k, :], W["hn"][:, k * 128:(k + 1) * 128], self.ident_b)
        self.cp("dve", W["hnT"], ptr)
        hnT = W["hnT"]
        if self.stop < 1.25:
            return

        def proj_tm(bank, c0, n):
            ps = self.bank(bank, [128, n])
            for k in range(8):
                self.mm(ps, hnT[:, k, :], win[:, k, c0:c0 + n], k == 0, k == 7)
            return ps

        def proj_fm(bank, col0, ncol, slot):
            ps = self.bank(bank, [128, 4, 128])[0:ncol, slot, :]
            for k in range(8):
                self.mm(ps, win[:, k, col0:col0 + ncol], hnT[:, k, :], k == 0, k == 7)
            return ps

        pqk = proj_tm(0, 0, 512)
        self.cp("act", W["qk"], pqk)
        self.tt("dve", W["qk2"], W["qk"], W["qk"], ALU.mult)
        self.rsum(W["qss"], W["qk2"].r("p (a b) -> p a b", b=64))
        self.rstd(W["qss"], W["qss"], 64, 8)
        self.tt("dve", W["qk2"].r("p (a b) -> p a b", b=64), W["qk"].r("p (a b) -> p a b", b=64),
                W["qss"].us(2).bc([128, 8, 64]), ALU.mult)
        self.tt("dve", W["qkn"], W["qk2"], X["qg"], ALU.mult)
        if self.stop < 1.35:
            return
        ptq = self.bank(6, [128, 4, 128], BF16)
        for i in range(4):
            self.tr(ptq[:, i, :], W["qkn"][:, i * 128:(i + 1) * 128], self.ident_b)
        if self.stop < 1.365:
            return
        self.cp("dve", X["qT"][:, :, j * 128:(j + 1) * 128], ptq[:, 0:2, :])
        if self.stop < 1.375:
            return
        self.ts("dve", X["nqT"][:, :, j * 128:(j + 1) * 128], ptq[:, 0:2, :], -1.0, None, ALU.mult)
        if self.stop < 1.385:
            return
        self.cp("dve", X["kT"][:, :, c * 128:(c + 1) * 128], ptq[:, 2:4, :])
        if self.stop < 1.45:
            return
        pv = proj_tm(1, 512, 256)
        self.cp("act", X["vv"][:, c, :], pv)

        if self.stop < 3:
            return
        U = X["U"]
        for g in range(2):
            for s in range(4):
                proj_fm(2 + g, 1280 + (g * 4 + s) * 128, 128, s)
            self.cp("act" if g == 0 else "dve", U[:, g * 4:(g + 1) * 4, 3:131], self.bank(2 + g, [128, 4, 128]))
        if c == 0:
            self.memset("pool", U[:, :, 3:3 + PADN], 0.0)
        Wd, Bd = X["Wd"], X["Bd"]
        accp = [self.bank(4, [128, 4, 128]), self.bank(5, [128, 4, 128])]
        for g in range(2):
            for s_ in range(4):
                kc = g * 4 + s_
                o_ = accp[g][:, s_, :]
                for t in range(4):
                    self.mm(o_, Wd[:, kc * 4 + t, :], U[:, kc, t:t + 128], t == 0, False)
                self.mm(o_, Bd[:, kc, :], self.ones_b, False, True)
        self.cp("dve", U[:, :, 0:3], U[:, :, 128:131])
        ex = W["ex"]
        for g in range(2):
            self.sigmoid(ex[:, g * 4:(g + 1) * 4, :], accp[g])
        self.tt("dve", ex[:, 0:4, :], accp[0], ex[:, 0:4, :], ALU.mult)
        self.tt("dve", W["BCb"], accp[1], ex[:, 4:8, :], ALU.mult)
        BCb = W["BCb"]
        pxt = self.bank(2, [128, 512])
        for k in range(4):
            self.tr(pxt[:, k * 128:(k + 1) * 128], ex[:, k, :], self.ident_f)
        self.cp("act", W["xtm"], pxt)
        pbt = self.bank(3, [128, 256], BF16)
        for g in range(2):
            self.tr(pbt[:, g * 128:(g + 1) * 128], BCb[:, g, :], self.ident_b)
        self.cp("dve", W["Btm"], pbt)
        pdt = proj_tm(1, 2304, 8)
        dt_, a_ = W["dt"], W["a"]
        self.tt("dve", dt_, pdt, X["dtb"], ALU.add)
        self.act(dt_, dt_, AF.Exp)
        self.act(dt_, dt_, AF.Ln, bias=self.one_c)
        if c == 0:
            self.memset("pool", dt_[0:PADN, :], 0.0)
        self.tt("dve", a_, dt_, X["aneg"], ALU.mult)
        psm = self.bank(1, [128, 16])
        self.mm(psm[:, 0:8], self.tri_incl_f, a_, True, True)
        self.mm(psm[:, 8:16], self.ones_f, a_, True, True)
        sm = W["sm"]
        self.act(sm[:, 0:8], psm[:, 0:8], AF.Exp)
        self.act(sm[:, 8:16], psm[:, 8:16], AF.Exp)
        self.tt("dve", sm[:, 16:24], psm[:, 8:16], psm[:, 0:8], ALU.subtract) if False else None
        self.cp("dve", sm[:, 24:32], psm[:, 0:8])
        self.tt("dve", sm[:, 16:24], psm[:, 8:16], sm[:, 24:32], ALU.subtract)
        self.act(sm[:, 16:24], sm[:, 16:24], AF.Exp)
        self.tt("dve", sm[:, 24:32], sm[:, 16:24], dt_, ALU.mult)
        xtm3 = W["xtm"].r("p (a b) -> p a b", b=64)
        self.tt("dve", W["xdt"].r("p (a b) -> p a b", b=64), xtm3, dt_.us(2).bc([128, 8, 64]), ALU.mult)
        self.tt("dve", W["xw"].r("p (a b) -> p a b", b=64), xtm3, sm[:, 24:32].us(2).bc([128, 8, 64]), ALU.mult)
        rhsA = W["rhsA"]
        self.tt("dve", rhsA, self.tri_incl_f.us(1).bc([128, 8, 128]), a_.us(2).bc([128, 8, 128]), ALU.mult)
        for half in range(2):
            pseg = self.bank(2 + half, [128, 4, 128])
            self.mm(pseg, self.low_strict_f, rhsA[:, half * 4:(half + 1) * 4, :], True, True)
            self.act(W["decT"][:, half * 4:(half + 1) * 4, :], pseg, AF.Exp)
        pcb = self.bank(5, [128, 2, 128])
        for g in range(2):
            self.mm(pcb[:, g, :], BCb[:, g, :], BCb[:, 2 + g, :], True, True)
        self.tt("dve", W["cbm"], pcb, self.mask_ge.us(1).bc([128, 2, 128]), ALU.mult)
        for g in range(2):
            self.tt("dve", W["MT"][:, g * 4:(g + 1) * 4, :], W["decT"][:, g * 4:(g + 1) * 4, :],
                    W["cbm"][:, g:g + 1, :].bc([128, 4, 128]), ALU.mult)
        pyd = self.bank(4, [128, 512])
        pyo = self.bank(0, [128, 512])
        for hh in range(8):
            self.mm(pyd[:, hh * 64:(hh + 1) * 64], W["MT"][:, hh, :], W["xdt"][:, hh * 64:(hh + 1) * 64], True, True)
        for g in range(2):
            self.mm(pyo[:, g * 256:(g + 1) * 256], BCb[:, 2 + g, :], X["stTb"][:, g * 256:(g + 1) * 256], True, True)
        pst = self.bank(1, [128, 512])
        for g in range(2):
            self.mm(pst[:, g * 256:(g + 1) * 256], W["Btm"][:, g * 128:(g + 1) * 128], W["xw"][:, g * 256:(g + 1) * 256], True, True)
        y = W["y"]
        stT = X["stT"]
        v3 = "p (a b) -> p a b"
        self.tt("dve", y.r(v3, b=64), pyo.r(v3, b=64), sm[:, 0:8].us(2).bc([128, 8, 64]), ALU.mult)
        self.tt("dve", y, y, pyd, ALU.add)
        self.tt("dve", stT.r(v3, b=64), stT.r(v3, b=64), sm[:, 8:16].us(2).bc([128, 8, 64]), ALU.mult)
        self.tt("dve", stT, stT, pst, ALU.add)
        self.cp("act", X["stTb"], stT)
        self.tt("dve", W["yoff"].r("p (a b) -> p a b", b=64), xtm3, X["Dsk"].us(2).bc([128, 8, 64]), ALU.mult)
        self.tt("dve", y, y, W["yoff"], ALU.add)
        pz = proj_tm(3, 768, 512)
        ez = W["ez"]
        self.sigmoid(ez, pz)
        self.tt("dve", ez, ez, pz, ALU.mult)
        self.tt("dve", y, y, ez, ALU.mult)
        self.tt("dve", ez, y, y, ALU.mult)
        self.rsum(W["oss"][:, 0:2], ez.r("p (a b) -> p a b", b=256))
        self.rstd(W["oss"][:, 0:2], W["oss"][:, 0:2], 256, 2)
        self.tt("dve", y.r("p (a b) -> p a b", b=256), y.r("p (a b) -> p a b", b=256),
                W["oss"][:, 0:2].us(2).bc([128, 2, 256]), ALU.mult)
        self.tt("dve", W["cc"][:, 0:512], y, X["snw"], ALU.mult)

        if self.stop < 4:
            return
        fm, fmb = W["fm"], W["fmb"]
        for hh in range(4):
            proj_fm(2, 2312 + hh * 64, 64, hh)
        for hh in range(4):
            proj_fm(3, 2568 + hh * 64, 64, hh)
        pq_ = T(self.bank(2, [128, 4, 128]).ap[0:64], "bank2")
        pf_ = T(self.bank(3, [128, 4, 128]).ap[0:64], "bank3")
        lb4 = X["lb"].us(2).bc([64, 4, 128])
        oml4 = X["oml"].us(2).bc([64, 4, 128])
        sq_, f_, g_, k_, t1, t2, t3 = fm
        self.sigmoid(t1, pq_)
        self.tt("dve", sq_, t1, pq_, ALU.mult)
        self.sigmoid(t1, pf_)
        self.tt("dve", t1, t1, oml4, ALU.mult)
        self.tt("dve", f_, t1, lb4, ALU.add)
        self.ts("dve", k_, f_, -1.0, 1.0, ALU.mult, ALU.add)
        self.ts("dve", f_, f_, TINY, None, ALU.max)
        self.act(f_, f_, AF.Ln)
        if c == 0:
            self.memset("pool", f_[:, :, 0:PADN], 0.0)
            self.memset("pool", k_[:, :, 0:PADN], 0.0)
        for hh in range(4):
            fo, fi = g_.ap[:, hh, :], f_.ap[:, hh, :]
            self.S.op("dve", lambda e, fo=fo, fi=fi: e.tensor_tensor_scan(out=fo, data0=fi, data1=fi, initial=0.0,
                                                                           op0=ALU.add, op1=ALU.bypass),
                      reads=[f_.res], writes=[g_.res])
        gref = g_[:, :, 63:64].bc([64, 4, 128])
        gtot = g_[:, :, 127:128].bc([64, 4, 128])
        self.tt("dve", t1, g_, gref, ALU.subtract)
        self.act(t2, t1, AF.Exp)
        self.tt("dve", fmb[0], sq_, t2, ALU.mult)
        self.act(t2, t1, AF.Exp, scale=-1.0)
        self.tt("dve", fmb[1], k_, t2, ALU.mult)
        self.tt("dve", X["kpad"][:, :, 64:128], k_[:, :, 64:128], t2[:, :, 64:128], ALU.mult)
        self.act(t2, g_, AF.Exp)
        self.tt("dve", fmb[2], sq_, t2, ALU.mult)
        self.tt("dve", t1, gtot, g_, ALU.subtract)
        self.act(t2, t1, AF.Exp)
        self.tt("dve", fmb[3], k_, t2, ALU.mult)
        self.act(t3[:, :, 0:1], g_[:, :, 127:128], AF.Exp)
        pke = self.bank(5, [128, 4, 64], BF16)
        for hh in range(4):
            self.tr(pke[:, hh, :], fmb[3][:, hh, :], self.ident_b[0:64, 0:64])
        self.cp("dve", W["ketm"], pke.r("p a b -> p (a b)"))
        pfi = proj_tm(0, 2824, 512)
        self.cp("dve", W["vtm"], pfi[:, 0:256])
        if c == 0:
            self.memset("pool", W["vtm"][0:PADN, :], 0.0)
        psc = self.bank(2, [128, 4, 128])
        scTm = X["scTm"]
        for hh in range(4):
            kt, qt = fmb[1][:, hh, :], fmb[0][:, hh, :]
            self.mm(psc[:, hh, 64:128], X["kpad"][:, hh, :], qt[:, 64:128], True, True)
            self.mm(psc[0:64, hh, 0:128], kt[:, 0:64], qt[:, 0:128], True, True)
        self.tt("dve", scTm[0:64, :, :], psc[0:64, :, :], self.mask_ge[0:64, :].us(1).bc([64, 4, 128]), ALU.mult)
        self.tt("dve", scTm[64:128, :, 64:128], psc[64:128, :, 64:128],
                self.mask_ge[64:128, 64:128].us(1).bc([64, 4, 64]), ALU.mult)
        po_ = self.bank(3, [128, 256])
        for hh in range(4):
            sl = slice(hh * 64, (hh + 1) * 64)
            self.mm(po_[:, sl], scTm[:, hh, :], W["vtm"][:, sl], True, False)
            self.mm(po_[:, sl], fmb[2][:, hh, :], X["Shb"][:, sl], False, True)
        pS = self.bank(1, [128, 256])
        pS = T(pS.ap[0:64], "bank1")
        for hh in range(4):
            sl = slice(hh * 64, (hh + 1) * 64)
            self.mm(pS[:, sl], W["ketm"][:, sl], W["vtm"][:, sl], True, True)
        Sh = X["Sh"]
        self.tt("dve", Sh.r("p (a b) -> p a b", b=64), Sh.r("p (a b) -> p a b", b=64),
                t3[:, :, 0:1].bc([64, 4, 64]), ALU.mult)
        self.tt("dve", Sh, Sh, pS, ALU.add)
        self.cp("act", X["Shb"], Sh)
        o, o2 = W["o"], W["o2"]
        self.cp("act", o, po_)
        self.tt("dve", o2, o, o, ALU.mult)
        self.rsum(W["oss"][:, 4:8], o2.r("p (a b) -> p a b", b=64))
        self.rstd(W["oss"][:, 4:8], W["oss"][:, 4:8], 64, 4)
        self.tt("dve", o.r("p (a b) -> p a b", b=64), o.r("p (a b) -> p a b", b=64),
                W["oss"][:, 4:8].us(2).bc([128, 4, 64]), ALU.mult)
        self.tt("dve", o, o, X["hnw"], ALU.mult)
        eg = W["eg"]
        self.sigmoid(eg, pfi[:, 256:512])
        self.tt("dve", eg, eg, pfi[:, 256:512], ALU.mult)
        self.tt("dve", W["cc"][:, 512:768], o, eg, ALU.mult)
        pct = self.bank(7, [128, 6, 128], BF16)
        for k in range(6):
            self.tr(pct[:, k, :], W["cc"][:, k * 128:(k + 1) * 128], self.ident_b)
        self.cp("dve", X["ccT"][:, :, j * 128:(j + 1) * 128], pct)

    def attention(self, X, sc):
        c0 = sc[0]
        nq = len(sc)
        N = nq * 128
        kT, vv, qT, nqT, AT = X["kT"], X["vv"], X["qT"], X["nqT"], X["AT"]
        blocks = list(range(c0 + nq - 1, -1, -1))
        for hp in range(2):
            heads = (2 * hp, 2 * hp + 1)
            pz = [self.bank(0 + s, [128, 512]) for s in range(2)]
            px = [self.bank(2 + s, [128, 512]) for s in range(2)]
            pop = self.bank(4, [128, 512])
            po = [T(pop.ap[64 * s:64 * s + 64], "bank4") for s in range(2)]

            def lo_of(sb):
                return max(0, (sb - c0) * 128)

            def kq(h, sb, lo, neg=False):
                r0 = (h % 2) * 64
                kb = kT[r0:r0 + 64, h // 2, sb * 128:(sb + 1) * 128]
                q = (nqT if neg else qT)[r0:r0 + 64, h // 2, lo:N]
                return kb, q

            def stageA(sb, s):
                lo = lo_of(sb)
                kb, q = kq(heads[s], sb, lo)
                self.mm(pz[s][:, lo:N], kb, q, True, True)

            def stageB(sb, s):
                lo = lo_of(sb)
                a = AT[s]
                self.act(a["e"][:, lo:N], pz[s][:, lo:N], AF.Exp)
                self.act(a["sp"][:, lo:N], a["e"][:, lo:N], AF.Ln, bias=self.one_c)
                if sb >= c0:
                    self.tt("dve", a["sp"][:, lo:lo + 128], a["sp"][:, lo:lo + 128], self.mask_gt_b, ALU.mult)
                if sb == 0:
                    self.memset("pool", a["sp"][0:PADN, lo:N], 0.0)

            def stageC1(sb, s):
                lo = lo_of(sb)
                kb, q = kq(heads[s], sb, lo)
                self.mm(px[s][:, lo:N], kb, q, sb == blocks[0], False, skip=True)

            def stageC2(sb, s):
                lo = lo_of(sb)
                self.mm(px[s][:, lo:N], self.negU_b, AT[s]["sp"][:, lo:N], False, False, skip=True)

            def stageD(sb, s):
                lo = lo_of(sb)
                a = AT[s]
                self.act(a["w"][:, lo:N], px[s][:, lo:N], AF.Exp)
                if sb >= c0:
                    self.tt("dve", a["w"][:, lo:lo + 128], a["w"][:, lo:lo + 128], self.mask_gt_b, ALU.mult)
                if sb == 0:
                    self.memset("pool", a["w"][0:PADN, lo:N], 0.0)

            def stageE1(sb, s):
                lo = lo_of(sb)
                if sb > 0:
                    kb, q = kq(heads[s], sb, lo, neg=True)
                    self.mm(px[s][:, lo:N], kb, q, False, False, skip=True)

            def stageE2(sb, s):
                h = heads[s]
                lo = lo_of(sb)
                if sb > 0:
                    self.mm(px[s][:, lo:N], self.negL_b, AT[s]["sp"][:, lo:N], False, False, skip=True)
                vb = vv[:, sb, h * 64:(h + 1) * 64]
                self.mm(po[s][:, lo:N], vb, AT[s]["w"][:, lo:N], sb == blocks[0], sb == 0, skip=True, tp=(0, 64 * s))

            for s in range(2):
                stageA(blocks[0], s)
            for r, sb in enumerate(blocks):
                for s in range(2):
                    stageB(sb, s)
                for s in range(2):
                    stageC1(sb, s)
                for s in range(2):
                    stageC2(sb, s)
                if r + 1 < len(blocks):
                    for s in range(2):
                        stageA(blocks[r + 1], s)
                for s in range(2):
                    stageD(sb, s)
                for s in range(2):
                    stageE1(sb, s)
                for s in range(2):
                    stageE2(sb, s)
            P = X["PT"]
            self.cp("act", P["o"][:, 0:N], pop[:, 0:N])
            self.tt("dve", P["o2"][:, 0:N], P["o"][:, 0:N], P["o"][:, 0:N], ALU.mult)
            pms = self.bank(6, [128, 512])
            self.mm(pms[:, 0:N], self.bd64_b, P["o2"][:, 0:N], True, True)
            self.act(P["r"][:, 0:N], pms[:, 0:N], AF.Ln, bias=self.eps_c)
            self.act(P["r"][:, 0:N], P["r"][:, 0:N], AF.Exp, scale=-0.5)
            self.stt(X["oTn"][:, hp, 0:N], P["o"][:, 0:N], X["sbw"][:, hp:hp + 1], P["r"][:, 0:N], ALU.mult, ALU.mult)

    def outproj(self, X, c, j):
        h = X["hres"]
        hmid = X["hmid"]
        tsl = slice(j * 128, (j + 1) * 128)
        l = X["l"]
        if l == 0:
            if c == 0:
                self.memset("pool", h, 0.0)
                self.dma(h[PADN:128, :], self.meta, key="hres")
            else:
                self.dma(h, self.x[(c - 1) * 128:c * 128, :], key="hres")
        else:
            self.dma(h, self.hbuf[c * 128:(c + 1) * 128, :], key="hres")
        for half in range(2):
            ps = self.bank(6 + half, [128, 512])
            ns = slice(half * 512, (half + 1) * 512)
            for k in range(2):
                self.mm(ps, X["oTn"][:, k, tsl], X["wo"][:, k, ns], k == 0, False)
            for k in range(6):
                self.mm(ps, X["ccT"][:, k, tsl], X["wo"][:, 2 + k, ns], False, k == 5)
            self.tt("dve", hmid[:, ns], ps, h[:, ns], ALU.add)
        self.dma(self.hbuf[c * 128:(c + 1) * 128, :], hmid, key="hres")

    def mlp_phase(self, l):
        NCH = self.NCH
        self.woff = 0
        self.koff = self.kbase
        aw, ak = self.aw, self.ak
        wup = aw("wup", [128, 8, DFF], BF16)
        wdn = aw("wdn", [128, 32, 1024], BF16)
        aT_region = ak("maTr", [128, 8192], F32)
        self.stg = [self.alias(aT_region, f"stg{q}", [128, 2048], F32, woff=2048 * q, res=f"mstg{q}_{l}")
                    for q in range(4)]
        i = 0
        for k in range(8):
            for hlf in range(2):
                self.load_cast(wup[:, k, hlf * 2048:(hlf + 1) * 2048],
                               self.w_up[l, k * 128:(k + 1) * 128, hlf * 2048:(hlf + 1) * 2048],
                               self.pk[:, l, 8 + k:9 + k], i)
                i += 1
        wdv = self.w_down[l].rearrange("(f p) n -> p f n", p=128)
        for f2 in range(16):
            stg = self.stg[i % 4]
            self.dma(stg.r("p (a b) -> p a b", a=2), wdv[:, f2 * 2:f2 * 2 + 2, :], key=f"stg{i % 4}",
                     eng=("sp" if i % 2 == 0 else "act"))
            self.cp(("dve", "act")[i % 2], wdn[:, f2 * 2:f2 * 2 + 2, :].r("p a b -> p (a b)"), stg)
            i += 1
        self.S.barrier()
        aT = self.alias(aT_region, "maT", [128, 32, 512], BF16)
        hs = [ak(f"mh{s}", [128, 1024], F32) for s in range(2)]
        hns = [ak(f"mhn{s}", [128, 1024], BF16) for s in range(2)]
        hnT = ak("mhnT", [128, 8, 512], BF16)
        hr = [ak(f"mr{s}", [128, 1024], F32) for s in range(2)]
        rl = [ak(f"mrl{s}", [128, 512], F32) for s in range(2)]
        sq = self.alias(rl[0], "msq", [128, 1024], BF16)
        ss = ak("mss", [128, 4], F32)
        last = (l == self.DEPTH - 1)
        tiles = []
        c = 1 if last else 0
        while c < NCH:
            n = min(4, NCH - c)
            tiles.append((c, n))
            c += n
        gbc = [0]

        def front(ti):
            c, n = tiles[ti]
            for b in range(n):
                gb = gbc[0]
                h = hs[gb % 2]
                hn = hns[gb % 2]
                self.dma(h, self.hbuf[(c + b) * 128:(c + b + 1) * 128, :], key=f"mh{gb % 2}")
                self.act(sq, h, AF.Square, accum=ss[:, b:b + 1])
                self.rstd(ss[:, b:b + 1], ss[:, b:b + 1], 1024, 1)
                self.ts("dve", hn, h, ss[:, b:b + 1], None, ALU.mult)
                ptr = self.bank(6 + b % 2, [128, 8, 128], BF16)
                for k in range(8):
                    self.tr(ptr[:, k, :], hn[:, k * 128:(k + 1) * 128], self.ident_b)
                self.cp("dve", hnT[:, :, b * 128:(b + 1) * 128], ptr)
                gbc[0] += 1

        def up(ti):
            c, n = tiles[ti]
            N = n * 128
            for f in range(32):
                ps = self.bank(f % 4, [128, 512])
                for k in range(8):
                    self.mm(ps[:, 0:N], wup[:, k, f * 128:(f + 1) * 128], hnT[:, k, 0:N], k == 0, k == 7)
                r = rl[f % 2]
                self.act(r[:, 0:N], ps[:, 0:N], AF.Relu)
                self.tt("dve", aT[:, f, 0:N], r[:, 0:N], r[:, 0:N], ALU.mult)

        def down(ti):
            c, n = tiles[ti]
            for b in range(n):
                cc = c + b
                hout = hr[b % 2]
                self.dma(hout, self.hbuf[cc * 128:(cc + 1) * 128, :], key=f"mr{b % 2}")
                for half in range(2):
                    ps = self.bank(4 + half, [128, 512])
                    ns = slice(half * 512, (half + 1) * 512)
                    for f in range(32):
                        self.mm(ps, aT[:, f, b * 128:(b + 1) * 128], wdn[:, f, ns], f == 0, f == 31)
                    self.tt("dve", hout[:, ns], ps, hout[:, ns], ALU.add)
                if last:
                    self.dma(self.out[(cc - 1) * 128:cc * 128, :], hout, key=f"mr{b % 2}")
                else:
                    self.dma(self.hbuf[cc * 128:(cc + 1) * 128, :], hout, key=f"mr{b % 2}")

        front(0)
        for ti in range(len(tiles)):
            up(ti)
            if ti + 1 < len(tiles):
                front(ti + 1)
            down(ti)


def prep_params(inp, depth=2):
    f = np.float32
    pk = np.zeros((128, 2, 56), f)
    pq = np.zeros((64, 2, 8), f)
    bcp = np.zeros((2, 1304), f)
    for l in range(depth):
        pk[:, l, 0:8] = inp["norm_mix_w"][l].reshape(8, 128).T
        pk[:, l, 8:16] = inp["norm_mlp_w"][l].reshape(8, 128).T
        cw = inp["ssd_conv_w"][l]
        pk[:, l, 16:48] = cw.reshape(4, 8, 128).transpose(2, 1, 0).reshape(128, 32)
        pk[:, l, 48:56] = inp["ssd_conv_b"][l].reshape(8, 128).T
        pq[:, l, 0:4] = inp["sb_out_norm"][l].T
        pq[:, l, 4:8] = inp["hg_lb_logits"][l].reshape(4, 64).T
        bcp[l, 0:256] = np.tile(inp["sb_q_norm"][l], 4)
        bcp[l, 256:512] = np.tile(inp["sb_k_norm"][l], 4)
        bcp[l, 512:520] = inp["ssd_dt_bias"][l]
        bcp[l, 520:528] = inp["ssd_A_log"][l]
        bcp[l, 528:536] = inp["ssd_D"][l]
        bcp[l, 536:1048] = inp["ssd_norm_w"][l].reshape(-1)
        bcp[l, 1048:1304] = inp["hg_out_norm"][l].reshape(-1)
    bcp = np.ascontiguousarray(np.broadcast_to(bcp[None], (128, 2, 1304)))
    pq2 = np.zeros((128, 2, 2), f)
    for l in range(depth):
        son = inp["sb_out_norm"][l]
        pq2[:, l, :] = son.reshape(2, 128).T
    return pk, pq, bcp, pq2


_CACHE = {}


def run(inputs, ncores, nch, depth):
    key = (nch, depth)
    if key not in _CACHE:
        _CACHE[key] = K(nch, depth).build()
    nc = _CACHE[key]
    pk, pq, bcp, pq2 = prep_params(inputs, depth)
    f = np.float32
    common = dict(meta=np.ascontiguousarray(inputs["meta_tokens"], f), w_in=np.ascontiguousarray(inputs["w_in"], f),
                  w_out=np.ascontiguousarray(inputs["w_out"], f), w_up=np.ascontiguousarray(inputs["w_up"], f),
                  w_down=np.ascontiguousarray(inputs["w_down"], f), pk=pk, pq=pq, bcp=bcp, pq2=pq2)
    in_maps = []
    for b in range(ncores):
        m = dict(common)
        m["x"] = np.ascontiguousarray(inputs["x"][b], f)
        in_maps.append(m)
    res = run_bass_kernel_spmd(nc, in_maps, core_ids=list(range(ncores)))
    return np.stack([np.asarray(r["out"]) for r in res.results], axis=0).astype(np.float32)


def kernel(**inputs):
    x = inputs["x"]
    B, S, _ = x.shape
    return run(inputs, B, S // 128 + 1, 2)
```
